# Optimizing a Trainium2 kernel written in Bass

```python
import jax, jax.numpy as jnp
from jax import lax
import numpy as np

D_MODEL = 1024
BATCH = 4
SEQ = 8192
DEPTH = 2
DEC_BATCH = 32
DEC_SEQ = 16
PAST_LEN = 2048

CHUNK = 64
WINDOW = 128
WINDOW_CHUNKS = WINDOW // CHUNK
HEAD_DIM = 64
ATTN_WIDTH = D_MODEL // 2
N_HEADS = ATTN_WIDTH // HEAD_DIM
N_KV_HEADS = max(1, N_HEADS // 4)
GROUP = N_HEADS // N_KV_HEADS
CONV_CH = D_MODEL - ATTN_WIDTH
CONV_WIDTH = 31
CONV_STATE = CONV_WIDTH - 1
ROT_DIM = HEAD_DIM // 4
ROPE_THETA = 500000.0
D_FF = 4 * D_MODEL
EPS = 1e-6
Q_COLS = N_HEADS * HEAD_DIM
KV_COLS = N_KV_HEADS * HEAD_DIM
IN_COLS = Q_COLS + 2 * KV_COLS + 2 * CONV_CH

kernel_name = 'hymba_swa_sink_conformer_conv_stream_step'


def rms_norm(x, g):
    x32 = x.astype(jnp.float32)
    y = x32 * lax.rsqrt(jnp.mean(x32 * x32, axis=-1, keepdims=True) + EPS)
    return (y * g.astype(jnp.float32)).astype(x.dtype)


def partial_rope(x, pos):
    half = ROT_DIM // 2
    inv = jnp.power(jnp.float32(ROPE_THETA), -jnp.arange(half, dtype=jnp.float32) * 2.0 / ROT_DIM)
    ang = pos.astype(jnp.float32)[:, None] * inv[None, :]
    cos = jnp.cos(ang)[:, None, :]
    sin = jnp.sin(ang)[:, None, :]
    x32 = x.astype(jnp.float32)
    x1 = x32[..., :half]
    x2 = x32[..., half:ROT_DIM]
    out = jnp.concatenate([x1 * cos - x2 * sin, x2 * cos + x1 * sin, x32[..., ROT_DIM:]], axis=-1)
    return out.astype(x.dtype)


def project(h, w_in, q_g, k_g, pos):
    B, T, _ = h.shape
    z = h @ w_in
    q, k, v, a, b = jnp.split(z, [Q_COLS, Q_COLS + KV_COLS, Q_COLS + 2 * KV_COLS,
                                  Q_COLS + 2 * KV_COLS + CONV_CH], axis=-1)
    q = partial_rope(rms_norm(q.reshape(B, T, N_HEADS, HEAD_DIM), q_g), pos)
    k = partial_rope(rms_norm(k.reshape(B, T, N_KV_HEADS, HEAD_DIM), k_g), pos)
    v = v.reshape(B, T, N_KV_HEADS, HEAD_DIM)
    u = a * jax.nn.sigmoid(b)
    return q, k, v, u


def sink_attention(q, k, v, sinks, valid):
    s = jnp.einsum('...qkgd,...jkd->...kgqj', q.astype(jnp.float32), k.astype(jnp.float32)) * (HEAD_DIM ** -0.5)
    if valid is not None:
        s = jnp.where(valid, s, -jnp.inf)
    sink = sinks.astype(jnp.float32).reshape(N_KV_HEADS, GROUP, 1, 1)
    m = jnp.maximum(jnp.max(s, axis=-1, keepdims=True), sink)
    p = jnp.exp(s - m)
    denom = jnp.sum(p, axis=-1, keepdims=True) + jnp.exp(sink - m)
    o = jnp.einsum('...kgqj,...jkd->...qkgd', p / denom, v.astype(jnp.float32))
    return o.astype(q.dtype)


def band_attention_prompt(q, k, v, sinks):
    B, S = q.shape[0], q.shape[1]
    nc = S // CHUNK
    qc = q.reshape(B, nc, CHUNK, N_KV_HEADS, GROUP, HEAD_DIM)
    pad = ((0, 0), (WINDOW, 0), (0, 0), (0, 0))
    kc = jnp.pad(k, pad).reshape(B, nc + WINDOW_CHUNKS, CHUNK, N_KV_HEADS, HEAD_DIM)
    vc = jnp.pad(v, pad).reshape(B, nc + WINDOW_CHUNKS, CHUNK, N_KV_HEADS, HEAD_DIM)
    kb = jnp.concatenate([kc[:, j:j + nc] for j in range(WINDOW_CHUNKS + 1)], axis=2)
    vb = jnp.concatenate([vc[:, j:j + nc] for j in range(WINDOW_CHUNKS + 1)], axis=2)
    key_pos = jnp.arange(nc)[:, None] * CHUNK - WINDOW + jnp.arange(WINDOW + CHUNK)[None, :]
    valid = (key_pos >= 0)[None, :, None, None, None, :]
    o = sink_attention(qc, kb, vb, sinks, valid)
    return o.reshape(B, S, ATTN_WIDTH)


def attention_sample(q, k_all, v_all, sinks):
    B, T = q.shape[0], q.shape[1]
    qg = q.reshape(B, T, N_KV_HEADS, GROUP, HEAD_DIM)
    o = sink_attention(qg, k_all, v_all, sinks, None)
    return o.reshape(B, T, ATTN_WIDTH)


def conv_tail(u_ext, w, b, ln_g, ln_b):
    y = lax.conv_general_dilated(u_ext, w[:, None, :], (1,), 'VALID',
                                 dimension_numbers=('NWC', 'WIO', 'NWC'),
                                 feature_group_count=CONV_CH) + b
    y32 = y.astype(jnp.float32)
    mu = jnp.mean(y32, axis=-1, keepdims=True)
    var = jnp.mean(jnp.square(y32 - mu), axis=-1, keepdims=True)
    yn = (y32 - mu) * lax.rsqrt(var + EPS) * ln_g.astype(jnp.float32) + ln_b.astype(jnp.float32)
    return jax.nn.silu(yn).astype(u_ext.dtype)


def finish_layer(x, attn_o, conv_o, beta_a, beta_c, w_out, g2, w_up, w_down):
    x = x + jnp.concatenate([attn_o * beta_a, conv_o * beta_c], axis=-1) @ w_out
    h = rms_norm(x, g2)
    return x + jnp.square(jax.nn.relu(h @ w_up)) @ w_down


def setup_inputs(seed: int = 0) -> dict:
    key = jax.random.key(seed)
    ks = jax.random.split(key, 24)
    f32 = jnp.float32
    nrm = lambda k, shape, s: jax.random.normal(k, shape, f32) * s
    return {
        'x_prompt': nrm(ks[0], (BATCH, SEQ, D_MODEL), 1.0),
        'x_sample': nrm(ks[1], (DEC_BATCH, DEC_SEQ, D_MODEL), 1.0),
        'cache_k': nrm(ks[2], (DEPTH, DEC_BATCH, WINDOW, N_KV_HEADS, HEAD_DIM), 1.0),
        'cache_v': nrm(ks[3], (DEPTH, DEC_BATCH, WINDOW, N_KV_HEADS, HEAD_DIM), 1.0),
        'state_conv': nrm(ks[4], (DEPTH, DEC_BATCH, CONV_STATE, CONV_CH), 0.5),
        'norm1_g': 1.0 + nrm(ks[5], (DEPTH, D_MODEL), 0.02),
        'w_in': nrm(ks[6], (DEPTH, D_MODEL, IN_COLS), D_MODEL ** -0.5),
        'q_norm_g': 1.0 + nrm(ks[7], (DEPTH, HEAD_DIM), 0.02),
        'k_norm_g': 1.0 + nrm(ks[8], (DEPTH, HEAD_DIM), 0.02),
        'attn_sinks': nrm(ks[9], (DEPTH, N_HEADS), 1.0),
        'conv_w': nrm(ks[10], (DEPTH, CONV_WIDTH, CONV_CH), CONV_WIDTH ** -0.5),
        'conv_b': nrm(ks[11], (DEPTH, CONV_CH), 0.02),
        'conv_ln_g': 1.0 + nrm(ks[12], (DEPTH, CONV_CH), 0.02),
        'conv_ln_b': nrm(ks[13], (DEPTH, CONV_CH), 0.02),
        'beta_attn': 1.0 + nrm(ks[14], (DEPTH, ATTN_WIDTH), 0.02),
        'beta_conv': 1.0 + nrm(ks[15], (DEPTH, CONV_CH), 0.02),
        'w_out': nrm(ks[16], (DEPTH, D_MODEL, D_MODEL), D_MODEL ** -0.5),
        'norm2_g': 1.0 + nrm(ks[17], (DEPTH, D_MODEL), 0.02),
        'w_up': nrm(ks[18], (DEPTH, D_MODEL, D_FF), D_MODEL ** -0.5),
        'w_down': nrm(ks[19], (DEPTH, D_FF, D_MODEL), D_FF ** -0.5),
    }


def reference(x_prompt, x_sample, cache_k, cache_v, state_conv, norm1_g, w_in, q_norm_g, k_norm_g,
              attn_sinks, conv_w, conv_b, conv_ln_g, conv_ln_b, beta_attn, beta_conv, w_out,
              norm2_g, w_up, w_down):
    S = x_prompt.shape[1]
    T = x_sample.shape[1]
    pos_p = jnp.arange(S)
    pos_s = PAST_LEN + jnp.arange(T)
    yp, ys = x_prompt, x_sample
    pk, pv, pc, sk, sv, sc = [], [], [], [], [], []
    for l in range(DEPTH):
        h = rms_norm(yp, norm1_g[l])
        q, k, v, u = project(h, w_in[l], q_norm_g[l], k_norm_g[l], pos_p)
        a_o = band_attention_prompt(q, k, v, attn_sinks[l])
        c_o = conv_tail(jnp.pad(u, ((0, 0), (CONV_STATE, 0), (0, 0))), conv_w[l], conv_b[l], conv_ln_g[l], conv_ln_b[l])
        yp = finish_layer(yp, a_o, c_o, beta_attn[l], beta_conv[l], w_out[l], norm2_g[l], w_up[l], w_down[l])
        pk.append(k[:, -WINDOW:])
        pv.append(v[:, -WINDOW:])
        pc.append(u[:, -CONV_STATE:])
        h = rms_norm(ys, norm1_g[l])
        q, k, v, u = project(h, w_in[l], q_norm_g[l], k_norm_g[l], pos_s)
        k_all = jnp.concatenate([cache_k[l].astype(k.dtype), k], axis=1)
        v_all = jnp.concatenate([cache_v[l].astype(v.dtype), v], axis=1)
        a_o = attention_sample(q, k_all, v_all, attn_sinks[l])
        u_ext = jnp.concatenate([state_conv[l].astype(u.dtype), u], axis=1)
        c_o = conv_tail(u_ext, conv_w[l], conv_b[l], conv_ln_g[l], conv_ln_b[l])
        ys = finish_layer(ys, a_o, c_o, beta_attn[l], beta_conv[l], w_out[l], norm2_g[l], w_up[l], w_down[l])
        sk.append(k_all[:, -WINDOW:])
        sv.append(v_all[:, -WINDOW:])
        sc.append(u_ext[:, -CONV_STATE:])
    return (yp, ys, jnp.stack(pk), jnp.stack(pv), jnp.stack(pc), jnp.stack(sk), jnp.stack(sv), jnp.stack(sc))
```

```python
import numpy as np
from contextlib import ExitStack
import concourse.bass as bass
import concourse.mybir as mybir
from concourse.bass_utils import run_bass_kernel_spmd

F32 = mybir.dt.float32
BF16 = mybir.dt.bfloat16
AF = mybir.ActivationFunctionType
ALU = mybir.AluOpType
AX = mybir.AxisListType

D = 1024; SEQ = 8192; NB = 4; DEPTH = 2; DB = 32; DT = 16; PAST = 2048
HD = 64; NH = 8; NKV = 2; CCH = 512; CW = 31; CST = 30; DFF = 4096
INC = 1792; EPS = 1e-6; THETA = 500000.0
NCORE = 8; MAIN = 4096; HALO = 256; NTOK = MAIN + HALO
NRING = 6; NDQ = 8
import os
USE_LN = int(os.environ.get('K_USE_LN', '1'))
ENGS = ('pe', 'act', 'dve', 'pool', 'sp')


class Sched:
    def __init__(self):
        self.streams = {e: [] for e in ENGS}
        self.cnt = {}
        self.res = {}
        self.known = {e: {} for e in ENGS}
        self.rr = {'sp': 0, 'pool': 0}

    def _need(self, eng, ev, waits):
        if ev is None:
            return
        sem, val = ev
        if eng == 'pe' and sem == 'pe':
            return
        if self.known[eng].get(sem, 0) >= val:
            return
        if waits.get(sem, 0) < val:
            waits[sem] = val

    def op(self, eng, fns, reads=(), writes=(), dma=False):
        if not isinstance(fns, (list, tuple)):
            fns = [fns]
        waits = {}
        for r in reads:
            st = self.res.get(r)
            if st:
                self._need(eng, st[0], waits)
        for w in writes:
            st = self.res.get(w)
            if st:
                self._need(eng, st[0], waits)
                for sem, val in st[1].items():
                    self._need(eng, (sem, val), waits)
        if dma:
            i = self.rr[eng]
            self.rr[eng] = (i + 1) % NDQ
            sem = 'd%s%d' % (eng, i)
            prev = self.cnt.get(sem, 0)
            if prev:
                self._need(eng, (sem, prev), waits)
            val = prev + 16
            inc = 16
        else:
            sem = eng
            val = self.cnt.get(sem, 0) + 1
            inc = 1
        self.cnt[sem] = val
        for s, v in waits.items():
            self.known[eng][s] = v
        for r in reads:
            st = self.res.setdefault(r, [None, {}])
            st[1][sem] = val
        for w in writes:
            self.res[w] = [(sem, val), {}]
        self.streams[eng].append((sorted(waits.items()), list(fns), sem, inc))


def MM(out, lhsT, rhs, start, stop):
    return lambda e: e.matmul(out, lhsT=lhsT, rhs=rhs, start=start, stop=stop)


def TR(out, in_, ident):
    return lambda e: e.transpose(out=out, in_=in_, identity=ident)


def ACT(out, in_, func, bias=None, scale=None):
    kw = {}
    if bias is not None:
        kw['bias'] = bias
    if scale is not None:
        kw['scale'] = scale
    return lambda e: e.activation(out=out, in_=in_, func=func, **kw)


def TT(out, in0, in1, op):
    return lambda e: e.tensor_tensor(out=out, in0=in0, in1=in1, op=op)


def STT(out, in0, scalar, in1, op0, op1):
    return lambda e: e.scalar_tensor_tensor(out=out, in0=in0, scalar=scalar, in1=in1, op0=op0, op1=op1)


def TS(out, in0, s1, s2, op0, op1=None):
    if op1 is None:
        return lambda e: e.tensor_scalar(out=out, in0=in0, scalar1=s1, scalar2=None, op0=op0)
    return lambda e: e.tensor_scalar(out=out, in0=in0, scalar1=s1, scalar2=s2, op0=op0, op1=op1)


def CP(out, in_):
    return lambda e: e.tensor_copy(out=out, in_=in_)


def ACPY(out, in_):
    return lambda e: e.copy(out=out, in_=in_)


def RCP(out, in_):
    return lambda e: e.reciprocal(out=out, in_=in_)


def MS(ap, v):
    return lambda e: e.memset(ap, v)


def DMA(out, in_):
    return lambda e: e.dma_start(out=out, in_=in_)


def build_program(n_main_tiles=8, do_sample=True, dbg=None):
    nc = bass.Bass("TRN2", target_bir_lowering=False)
    S = Sched()

    def din(name, shape, dt=F32):
        return nc.dram_tensor(name, list(shape), dt, kind="ExternalInput").ap()

    def dout(name, shape):
        return nc.dram_tensor(name, list(shape), F32, kind="ExternalOutput").ap()

    x_d = din("x", [NTOK, D]); xs_d = din("xs", [64, D])
    ck_d = din("ck", [2, 4, 128, 128]); cv_d = din("cv", [2, 4, 128, 128]); stc_d = din("stc", [2, 4, CST, CCH])
    wi_d = din("w_in", [2, D, INC]); wo_d = din("w_out", [2, D, D]); wu_d = din("w_up", [2, D, DFF]); wd_d = din("w_down", [2, DFF, D])
    ident_d = din("ident", [128, 128]); g1_d = din("g1T", [128, 2, 8]); g2_d = din("g2T", [128, 2, 8])
    g10_d = din("g10", [128, 2, 640]); snk_d = din("snk", [128, 2, 8]); cw_d = din("cwT", [128, 2, 4, CW])
    cvec_d = din("cvec", [128, 2, 4, 4]); ba_d = din("ba", [64, 2, 8]); csn_d = din("csn", [128, NTOK // 128, 16])
    css_d = din("css", [64, 16]); hbf_d = din("hbf", [128, 2]); msk_d = din("msk", [64, 64])
    y_d = dout("y", [MAIN, D]); ys_d = dout("ys", [64, D])
    pk_d = dout("pk", [2, 128, 128]); pv_d = dout("pv", [2, 128, 128]); pc_d = dout("pc", [2, CST, CCH])
    sk_d = dout("sk", [2, 4, 128, 128]); sv_d = dout("sv", [2, 4, 128, 128]); sc_d = dout("sc", [2, 4, CST, CCH])
    wib = nc.dram_tensor("wib", [2, D, INC], BF16, kind="Internal").ap()
    wob = nc.dram_tensor("wob", [2, D, D], BF16, kind="Internal").ap()
    wub = nc.dram_tensor("wub", [2, D, DFF], BF16, kind="Internal").ap()
    wdb = nc.dram_tensor("wdb", [2, DFF, D], BF16, kind="Internal").ap()

    es = ExitStack()
    with es:
        def sb(name, shape, dt=F32):
            return es.enter_context(nc.sbuf_tensor(name, list(shape), dt))

        def ps(name):
            return es.enter_context(nc.psum_tensor(name, [128, 1024], F32))

        xT = sb("xT", [128, 8, 512]); hT = sb("hT", [128, 8, 512], BF16); ar = sb("ar", [128, 32, 512], BF16)
        kT = sb("kT", [64, 2, 2, 640], BF16); vS = sb("vS", [128, 2, 5, 2, 128], BF16); uX = sb("uX", [128, 2, 4, 544], BF16)
        ring = sb("ring", [128, NRING, 4096], BF16); stg = sb("stg", [128, 2, 1024]); dg = sb("dg", [128, 2, CW, 128], BF16)
        pT = sb("pT", [128, 2, 2, 512], BF16)
        t_rs = sb("t_rs", [128, 512]); t_rstd = sb("t_rstd", [128, 512]); t_sg = sb("t_sg", [128, 512])
        t_mu = sb("t_mu", [128, 512]); t_a = sb("t_a", [128, 512]); t_w = sb("t_w", [128, 512]); t_yn = sb("t_yn", [128, 512])
        t_den = sb("t_den", [128, 512]); t_rden = sb("t_rden", [128, 512])
        t_r = sb("t_r", [128, 2, 512], BF16)
        sqq2 = sb("sqq", [128, 2, 640]); qkn2 = sb("qkn", [128, 2, 640]); qkb2 = sb("qkb", [128, 2, 640], BF16)
        ss102 = sb("ss10", [128, 2, 10]); rs102 = sb("rs10", [128, 2, 10]); rq102 = sb("rq10", [128, 2, 10])
        rp2 = sb("rp", [128, 2, 4, 80])
        v32 = sb("v32", [128, 128]); u32 = sb("u32", [128, 4, 64]); ostg = sb("ostg", [64, 512])
        identf = sb("identf", [128, 128]); identb = sb("identb", [128, 128], BF16)
        onesf = sb("onesf", [128, 128]); onesb = sb("onesb", [128, 128], BF16)
        g1T = sb("g1T_s", [128, 2, 8]); g2T = sb("g2T_s", [128, 2, 8]); g10 = sb("g10_s", [128, 2, 640])
        snk = sb("snk_s", [128, 2, 8]); esink = sb("esink", [128, 2, 8]); cwT = sb("cwT_s", [128, 2, 4, CW])
        cvec = sb("cvec_s", [128, 2, 4, 4]); ba = sb("ba_s", [64, 2, 8]); csn = sb("csn_s", [128, NTOK // 128, 16])
        css = sb("css_s", [64, 16]); hbf = sb("hbf_s", [128, 2]); mskf = sb("mskf", [64, 64]); mskb = sb("mskb", [64, 64], BF16)
        zcol = sb("zcol", [128, 1]); ecol = sb("ecol", [128, 1])
        kTc = sb("kTc", [64, 4, 2, 128], BF16); vc = sb("vc", [128, 4, 2, 128], BF16); usx = sb("usx", [128, 4, 4, 48], BF16)
        ckb = sb("ckb", [128, 4, 128], BF16); pTc = sb("pTc", [128, 256], BF16); pTn = sb("pTn", [64, 256], BF16)
        P01 = ps("P01"); P23 = ps("P23"); P45 = ps("P45"); P67 = ps("P67")
        banks = [P01[:, 0:512], P01[:, 512:1024], P23[:, 0:512], P23[:, 512:1024],
                 P45[:, 0:512], P45[:, 512:1024], P67[:, 0:512], P67[:, 512:1024]]
        PQ = P45; PT = P67
        PTb = P67[:, :].bitcast(BF16)

        def bk(i):
            return ('ps', i)

        qT = ar[0:64, 0:8, :]; co = ar[:, 16:20, :]

        def f32view(lo, n):
            return ar[:, lo:lo + 2 * n, :].rearrange("p a b -> p (a b)").bitcast(F32).rearrange("p (c t) -> p c t", c=n)
        y32 = f32view(20, 4)
        def arr(lo, hi):
            return [('ar', i) for i in range(lo, hi)]

        sem_names = list(ENGS[:4]) + ['dsp%d' % i for i in range(NDQ)] + ['dpool%d' % i for i in range(NDQ)]
        sems = {n: es.enter_context(nc.semaphore("s_" + n)) for n in sem_names}

        def load_const(dst, src, key):
            S.op('sp', DMA(dst, src), writes=[key], dma=True)
        load_const(identf[:], ident_d[:, :], 'identf'); load_const(g1T[:], g1_d[:, :, :], 'g1T'); load_const(g2T[:], g2_d[:, :, :], 'g2T')
        load_const(g10[:], g10_d[:, :, :], 'g10'); load_const(snk[:], snk_d[:, :, :], 'snk'); load_const(cwT[:], cw_d[:, :, :, :], 'cwT')
        load_const(cvec[:], cvec_d[:, :, :, :], 'cvec'); load_const(ba[:], ba_d[:, :, :], 'ba'); load_const(csn[:], csn_d[:, :, :], 'csn')
        load_const(css[:], css_d[:, :], 'css'); load_const(hbf[:], hbf_d[:, :], 'hbf'); load_const(mskf[:], msk_d[:, :], 'mskf')
        if do_sample:
            S.op('pool', DMA(sk_d[:, :, 0:112, :], ck_d[:, :, 16:128, :]), dma=True)
            S.op('pool', DMA(sv_d[:, :, 0:112, :], cv_d[:, :, 16:128, :]), dma=True)
            S.op('pool', DMA(sc_d[:, :, 0:14, :], stc_d[:, :, 16:30, :]), dma=True)
        S.op('dve', CP(identb[:], identf[:]), reads=['identf'], writes=['identb'])
        S.op('dve', CP(mskb[:], mskf[:]), reads=['mskf'], writes=['mskb'])
        S.op('dve', MS(onesf[:], 1.0), writes=['onesf']); S.op('dve', MS(onesb[:], 1.0), writes=['onesb'])
        S.op('dve', MS(zcol[:], 0.0), writes=['zcol']); S.op('dve', MS(ecol[:], EPS), writes=['ecol'])
        S.op('dve', MS(kT[:].rearrange("p a b c -> p (a b c)"), 0.0), writes=[('kT', 0), ('kT', 1)])
        S.op('dve', MS(vS[:].rearrange("p a b c d -> p (a b c d)"), 1.0), writes=[('vS', l, b) for l in range(2) for b in range(5)])
        S.op('dve', MS(vS[:, :, 0, :, 0:64], 0.0), writes=[('vS', l, 0) for l in range(2)])
        S.op('dve', MS(uX[:].rearrange("p a b c -> p (a b c)"), 0.0), writes=[('uX', l, c) for l in range(2) for c in range(4)])
        S.op('dve', MS(pT[:].rearrange("p a b c -> p (a b c)"), 0.0), writes=[('pT', i, j) for i in range(2) for j in range(2)])
        S.op('dve', MS(vc[:].rearrange("p a b c -> p (a b c)"), 1.0), writes=['vc']); S.op('dve', MS(usx[:].rearrange("p a b c -> p (a b c)"), 0.0), writes=['usx'])
        S.op('act', ACT(esink[:], snk[:], AF.Exp), reads=['snk'], writes=['esink'])

        chunks = []

        def add_layer_chunks(l, front_only=False):
            def kview(n):
                return lambda s: ring[:, s, 0:8 * n].rearrange("p (k c) -> p k c", k=8)
            for ci, (c0, c1) in enumerate(((0, 512), (512, 768), (768, 1280), (1280, 1792))):
                chunks.append((wib[l, :, c0:c1].rearrange("(k p) c -> p k c", p=128), kview(c1 - c0), ('wbc', l, ci),
                               wib[l, :, c0:c1], wi_d[l, :, c0:c1]))
            if front_only:
                return
            for ci, r0 in enumerate((0, 512)):
                chunks.append((wob[l, r0:r0 + 512, :].rearrange("(k p) c -> p k c", p=128),
                               lambda s: ring[:, s, 0:4096].rearrange("p (k c) -> p k c", k=4), ('wbc', l, 4 + ci),
                               wob[l, r0:r0 + 512, :], wo_d[l, r0:r0 + 512, :]))
            for j in range(8):
                chunks.append((wub[l, :, j * 512:(j + 1) * 512].rearrange("(k p) c -> p k c", p=128), kview(512), ('wbc', l, 6 + j),
                               wub[l, :, j * 512:(j + 1) * 512], wu_d[l, :, j * 512:(j + 1) * 512]))
            for ch in range(2):
                for kc in range(4):
                    chunks.append((wdb[l, kc * 1024:(kc + 1) * 1024, ch * 512:(ch + 1) * 512].rearrange("(k p) c -> p k c", p=128),
                                   kview(512), ('wbc', l, 14 + ch * 4 + kc),
                                   wdb[l, kc * 1024:(kc + 1) * 1024, ch * 512:(ch + 1) * 512],
                                   wd_d[l, kc * 1024:(kc + 1) * 1024, ch * 512:(ch + 1) * 512]))
        add_layer_chunks(0); add_layer_chunks(1, front_only=True)
        for _t in range(n_main_tiles):
            add_layer_chunks(0); add_layer_chunks(1)
        if do_sample:
            add_layer_chunks(0); add_layer_chunks(1)
        cast_done = set()
        wst = {'issued': 0, 'next': 0, 'released': 0}

        def wpump():
            while wst['issued'] < len(chunks) and wst['issued'] - wst['released'] < NRING:
                j = wst['issued']
                src, vf, key, breg, freg = chunks[j]
                s = j % NRING
                if key not in cast_done:
                    cast_done.add(key)
                    S.op('pool', DMA(breg, freg), writes=[key], dma=True)
                S.op('sp', DMA(vf(s), src), reads=[key], writes=[('ring', s)], dma=True)
                wst['issued'] += 1

        def wnext():
            i = wst['next']
            wpump()
            assert wst['issued'] > i, "weight ring: too many chunks held"
            wst['next'] += 1
            s = i % NRING
            return chunks[i][1](s), ('ring', s)

        def wdone(n=1):
            wst['released'] += n
            assert wst['released'] <= wst['next']
            wpump()

        OT2 = ar[:, 8:12, :]
        sq = ar[:, 0:8, :]
        SSB = 7

        def norm_stats_k(k, T):
            S.op('act', ACT(sq[:, k, :T], xT[:, k, :T], AF.Square), reads=[('xT', k)], writes=[('ar', k)])
            S.op('pe', MM(banks[SSB][:, :T], onesb[:], sq[:, k, :T], k == 0, k == 7), reads=[('ar', k), 'onesb'], writes=[bk(SSB)])

        def norm_finish(l, gT, T):
            if USE_LN:
                S.op('act', ACT(t_rs[:, :T], banks[SSB][:, :T], AF.Ln, bias=ecol[:], scale=1.0 / D), reads=[bk(SSB), 'ecol'], writes=['t_rs'])
                S.op('act', ACT(t_rstd[:, :T], t_rs[:, :T], AF.Exp, bias=zcol[:], scale=-0.5), reads=['t_rs', 'zcol'], writes=['t_rstd'])
            else:
                S.op('act', ACT(t_rs[:, :T], banks[SSB][:, :T], AF.Sqrt, bias=ecol[:], scale=1.0 / D), reads=[bk(SSB), 'ecol'], writes=['t_rs'])
                S.op('dve', RCP(t_rstd[:, :T], t_rs[:, :T]), reads=['t_rs'], writes=['t_rstd'])
            for k in range(8):
                S.op('dve', STT(hT[:, k, :T], xT[:, k, :T], gT[:, l, k:k + 1], t_rstd[:, :T], ALU.mult, ALU.mult),
                     reads=[('xT', k), 't_rstd', 'g1T', 'g2T'], writes=[('hT', k)])

        def norm_phase(l, gT, T):
            for k in range(8):
                if k % 2 == 1:
                    S.op('dve', TT(sq[:, k, :T], xT[:, k, :T], xT[:, k, :T], ALU.mult), reads=[('xT', k)], writes=[('ar', k)])
                    S.op('pe', MM(banks[SSB][:, :T], onesb[:], sq[:, k, :T], k == 0, k == 7), reads=[('ar', k), 'onesb'], writes=[bk(SSB)])
                else:
                    norm_stats_k(k, T)
            norm_finish(l, gT, T)

        hTk = [('hT', k) for k in range(8)]

        def front_phase(l, T, nb, bs, sample, blk0, want_kv_out, tail, defer_last=False, build_dg=True):
            wq, kq = wnext(); wkv, kkv = wnext(); wa, ka = wnext(); wb_, kb = wnext()
            PQs = (P01, P23)

            def qkv_A(tb):
                c0 = tb * bs; pr = tb % 2
                PQ = PQs[pr]; bq = [bk(2 * pr), bk(2 * pr + 1)]
                sqq = sqq2[:, pr, :]; qkn = qkn2[:, pr, :]; qkb = qkb2[:, pr, :]
                ss10 = ss102[:, pr, :]; rs10 = rs102[:, pr, :]; rq10 = rq102[:, pr, :]
                K = lambda n: (n, pr)
                fq = [MM(PQ[:bs, 0:512], hT[:, k, c0:c0 + bs], wq[:, k, :], k == 0, k == 7) for k in range(8)]
                fkv = [MM(PQ[:bs, 512:768], hT[:, k, c0:c0 + bs], wkv[:, k, :], k == 0, k == 7) for k in range(8)]
                if tb == 0:
                    for k in range(8):
                        S.op('pe', fq[k], reads=[('hT', k), kq], writes=bq)
                    S.op('pe', fkv, reads=hTk + [kkv], writes=bq)
                else:
                    S.op('pe', fq + fkv, reads=hTk + [kq, kkv], writes=bq)
                S.op('act', ACT(sqq[:bs, :], PQ[:bs, 0:640], AF.Square), reads=bq, writes=[K('sqq')])
                S.op('dve', lambda e: e.tensor_reduce(out=ss10[:bs, :], in_=sqq[:bs, :].rearrange("p (h d) -> p h d", h=10), axis=AX.X, op=ALU.add),
                     reads=[K('sqq')], writes=[K('ss10')])
                S.op('act', ACT(rs10[:bs, :], ss10[:bs, :], AF.Ln, bias=ecol[:bs, :], scale=1.0 / HD), reads=[K('ss10'), 'ecol'], writes=[K('rs10')])
                S.op('act', ACT(rq10[:bs, :], rs10[:bs, :], AF.Exp, bias=zcol[:bs, :], scale=-0.5), reads=[K('rs10'), 'zcol'], writes=[K('rq10')])
                q3 = qkn[:bs, :].rearrange("p (h d) -> p h d", h=10)
                S.op('dve', TT(q3, PQ[:bs, 0:640].rearrange("p (h d) -> p h d", h=10), rq10[:bs, :].unsqueeze(2).to_broadcast([bs, 10, 64]), ALU.mult),
                     reads=bq + [K('rq10')], writes=[K('qkn')])
                S.op('act', ACPY(vS[:bs, l, tb + 1, :, 0:64], PQ[:bs, 640:768].rearrange("p (k d) -> p k d", k=2)),
                     reads=bq, writes=[('vS', l, tb + 1)])
                kvout = want_kv_out and (sample or tb == nb - 1)
                if kvout:
                    S.op('act', ACPY(v32[:bs, :], PQ[:bs, 640:768]), reads=bq, writes=['v32'])
                S.op('dve', TT(qkn[:bs, :], qkn[:bs, :], g10[:bs, l, :], ALU.mult), reads=[K('qkn'), 'g10'], writes=[K('qkn')])
                if sample:
                    cs_ = css[:bs, 0:8]; sn_ = css[:bs, 8:16]
                else:
                    cs_ = csn[:bs, blk0 + tb, 0:8]; sn_ = csn[:bs, blk0 + tb, 8:16]
                cosb = cs_.unsqueeze(1).to_broadcast([bs, 10, 8]); sinb = sn_.unsqueeze(1).to_broadcast([bs, 10, 8])
                x1 = q3[:, :, 0:8]; x2 = q3[:, :, 8:16]
                r = [rp2[:bs, pr, i, :].rearrange("p (h d) -> p h d", h=10) for i in range(4)]
                S.op('dve', [TT(r[0], x1, cosb, ALU.mult), TT(r[1], x2, sinb, ALU.mult), TT(r[2], x2, cosb, ALU.mult), TT(r[3], x1, sinb, ALU.mult)],
                     reads=[K('qkn'), 'csn', 'css'], writes=[K('rp')])
                S.op('dve', [TT(x1, r[0], r[1], ALU.subtract), TT(x2, r[2], r[3], ALU.add)], reads=[K('rp')], writes=[K('qkn')])
                S.op('act', ACPY(qkb[:bs, :], qkn[:bs, :]), reads=[K('qkn')], writes=[K('qkb')])
                if kvout:
                    if sample:
                        for s in range(4):
                            S.op('pool', DMA(sk_d[l, s, 112:128, :], qkn[s * 16:(s + 1) * 16, 512:640]), reads=[K('qkn')], dma=True)
                            S.op('pool', DMA(sv_d[l, s, 112:128, :], v32[s * 16:(s + 1) * 16, :]), reads=['v32'], dma=True)
                    else:
                        S.op('pool', DMA(pk_d[l, :, :], qkn[:, 512:640]), reads=[K('qkn')], dma=True)
                        S.op('pool', DMA(pv_d[l, :, :], v32[:, :]), reads=['v32'], dma=True)

            def qkv_B(tb):
                c0 = tb * bs; pr = tb % 2
                qkb = qkb2[:, pr, :]
                S.op('pe', [TR(PTb[0:64, hd * 128:hd * 128 + bs], qkb[:bs, hd * 64:(hd + 1) * 64], identb[:bs, :bs]) for hd in range(10)],
                     reads=[('qkb', pr), 'identb'], writes=[bk(6), bk(7)])
                S.op('dve', CP(qT[:, :, c0:c0 + bs], PTb[0:64, 0:1024].rearrange("p (h t) -> p h t", h=8)[:, :, :bs]),
                     reads=[bk(6), bk(7)], writes=arr(0, 8))
                S.op('dve', CP(kT[:, l, :, 128 + c0:128 + c0 + bs], PTb[0:64, 1024:1280].rearrange("p (h t) -> p h t", h=2)[:, :, :bs]),
                     reads=[bk(6), bk(7)], writes=[('kT', l)])

            def glu_ct(ct):
                ia = 4; ib = 5
                pa = banks[ia]; pb = banks[ib]
                S.op('pe', [MM(pa[:, :T], wa[:, k, ct * 128:(ct + 1) * 128], hT[:, k, :T], k == 0, k == 7) for k in range(8)],
                     reads=hTk + [ka], writes=[bk(ia)])
                S.op('pe', [MM(pb[:, :T], wb_[:, k, ct * 128:(ct + 1) * 128], hT[:, k, :T], k == 0, k == 7) for k in range(8)],
                     reads=hTk + [kb], writes=[bk(ib)])
                S.op('act', ACT(t_sg[:, :T], pb[:, :T], AF.Sigmoid), reads=[bk(ib)], writes=['t_sg'])
                if sample:
                    S.op('dve', TT(usx[:, ct, :, 32:48], pa[:, :64].rearrange("p (s t) -> p s t", s=4),
                                   t_sg[:, :64].rearrange("p (s t) -> p s t", s=4), ALU.mult), reads=[bk(ia), 't_sg'], writes=['usx'])
                else:
                    S.op('dve', TT(uX[:, l, ct, 32:32 + T], pa[:, :T], t_sg[:, :T], ALU.mult), reads=[bk(ia), 't_sg'], writes=[('uX', l, ct)])
                if tail:
                    S.op('dve', TT(u32[:, ct, 0:tail], pa[:, T - tail:T], t_sg[:, T - tail:T], ALU.mult), reads=[bk(ia), 't_sg'], writes=['u32'])

            if build_dg:
                dg_build(l, 0); dg_build(l, 1)
            for i in range(4):
                if i < nb:
                    qkv_A(i)
                glu_ct(i)
                if i >= 1 and i - 1 < nb:
                    qkv_B(i - 1)
            wdone(4)
            if nb == 4:
                if defer_last:
                    return lambda: qkv_B(3)
                qkv_B(3)
            return None

        def attn_epilogue(l, kvh, Ob, ibk, ncol, qlo, nq, sample=False):
            es_ = esink[64:128, l, kvh * 4:(kvh + 1) * 4]
            if sample:
                v4 = lambda ap: ap.rearrange("p (s g q) -> p s g q", s=4, g=4)
                S.op('dve', TT(v4(t_den[0:64, :ncol]), v4(Ob[64:128, :ncol]), es_.unsqueeze(1).unsqueeze(3).to_broadcast([64, 4, 4, 16]), ALU.add),
                     reads=[bk(ibk), 'esink'], writes=['t_den'])
            else:
                v3 = lambda ap: ap.rearrange("p (g q) -> p g q", g=4)
                S.op('dve', TT(v3(t_den[0:64, :ncol]), v3(Ob[64:128, :ncol]), es_.unsqueeze(2).to_broadcast([64, 4, nq]), ALU.add),
                     reads=[bk(ibk), 'esink'], writes=['t_den'])
            if USE_LN:
                S.op('act', ACT(t_den[0:64, :ncol], t_den[0:64, :ncol], AF.Ln, bias=zcol[0:64, :], scale=1.0), reads=['t_den', 'zcol'], writes=['t_den'])
                S.op('act', ACT(t_rden[0:64, :ncol], t_den[0:64, :ncol], AF.Exp, bias=zcol[0:64, :], scale=-1.0), reads=['t_den', 'zcol'], writes=['t_rden'])
            else:
                S.op('dve', RCP(t_rden[0:64, :ncol], t_den[0:64, :ncol]), reads=['t_den'], writes=['t_rden'])
            for g in range(4):
                h = kvh * 4 + g
                po = (h % 2) * 64
                if sample:
                    src = Ob[0:64, :ncol].rearrange("p (s g q) -> p s g q", s=4, g=4)[:, :, g, :]
                    rd_ = t_rden[0:64, :ncol].rearrange("p (s g q) -> p s g q", s=4, g=4)[:, :, g, :]
                    dst = OT2[po:po + 64, h // 2, 0:64].rearrange("p (s q) -> p s q", s=4)
                else:
                    src = Ob[0:64, g * nq:(g + 1) * nq]; rd_ = t_rden[0:64, g * nq:(g + 1) * nq]; dst = OT2[po:po + 64, h // 2, qlo:qlo + nq]
                S.op('dve', STT(dst, src, ba[:, l, h:h + 1], rd_, ALU.mult, ALU.mult),
                     reads=[bk(ibk), 't_rden', 'ba'], writes=[('ar', 8 + h // 2)])

        y16 = ar[:, 28:32, :]
        ysqb = ar[:, 12:16, :]

        def dg_build(l, ct):
            d = ct % 2
            S.op('pool', TT(dg[:, d, :, :], identb[:].unsqueeze(1).to_broadcast([128, CW, 128]),
                            cwT[:, l, ct, :].unsqueeze(2).to_broadcast([128, CW, 128]), ALU.mult),
                 reads=['identb', 'cwT'], writes=[('dg', d)])

        def conv_mm(l, T, sample, ct):
            d = ct % 2
            Y = banks[6 + d]
            if sample:
                fns = [MM(Y[:, 0:64], dg[:, d, j, :], usx[:, ct, :, 2 + j:2 + j + 16], j == 0, j == CW - 1) for j in range(CW)]
                rd = ['usx']
            else:
                fns = [MM(Y[:, :T], dg[:, d, j, :], uX[:, l, ct, 2 + j:2 + j + T], j == 0, j == CW - 1) for j in range(CW)]
                rd = [('uX', l, ct)]
            S.op('pe', fns, reads=rd + [('dg', d)], writes=[bk(6 + d)])
            if ct + 2 < 4:
                dg_build(l, ct + 2)

        def conv_evac(l, T, ct):
            d = ct % 2
            Y = banks[6 + d]
            S.op('act', ACT(y32[:, ct, :T], Y[:, :T], AF.Identity, bias=cvec[:, l, ct, 0:1]), reads=[bk(6 + d), 'cvec'], writes=arr(20 + 2 * ct, 22 + 2 * ct))
            S.op('act', ACT(ysqb[:, ct, :T], Y[:, :T], AF.Square, bias=cvec[:, l, ct, 0:1]), reads=[bk(6 + d), 'cvec'], writes=[('ar', 12 + ct)])
            S.op('act', ACT(y16[:, ct, :T], Y[:, :T], AF.Identity, bias=cvec[:, l, ct, 0:1]), reads=[bk(6 + d), 'cvec'], writes=[('ar', 28 + ct)])

        def ln_stats(l, T):
            S.op('pe', [MM(banks[6][:, :T], onesb[:], y16[:, ct, :T], ct == 0, ct == 3) for ct in range(4)], reads=arr(28, 32) + ['onesb'], writes=[bk(6)])
            S.op('pe', [MM(banks[7][:, :T], onesb[:], ysqb[:, ct, :T], ct == 0, ct == 3) for ct in range(4)], reads=arr(12, 16) + ['onesb'], writes=[bk(7)])
            S.op('act', ACT(t_mu[:, :T], banks[6][:, :T], AF.Identity, bias=zcol[:], scale=1.0 / CCH), reads=[bk(6), 'zcol'], writes=['t_mu'])
            S.op('dve', TT(t_a[:, :T], t_mu[:, :T], t_mu[:, :T], ALU.mult), reads=['t_mu'], writes=['t_a'])
            S.op('dve', STT(t_a[:, :T], banks[7][:, :T], 1.0 / CCH, t_a[:, :T], ALU.mult, ALU.subtract), reads=[bk(7), 't_a'], writes=['t_a'])
            if USE_LN:
                S.op('act', ACT(t_rs[:, :T], t_a[:, :T], AF.Ln, bias=ecol[:], scale=1.0), reads=['t_a', 'ecol'], writes=['t_rs'])
                S.op('act', ACT(t_rstd[:, :T], t_rs[:, :T], AF.Exp, bias=zcol[:], scale=-0.5), reads=['t_rs', 'zcol'], writes=['t_rstd'])
            else:
                S.op('act', ACT(t_rs[:, :T], t_a[:, :T], AF.Sqrt, bias=ecol[:], scale=1.0), reads=['t_a', 'ecol'], writes=['t_rs'])
                S.op('dve', RCP(t_rstd[:, :T], t_rs[:, :T]), reads=['t_rs'], writes=['t_rstd'])

        def ln_apply(l, T, ct, alt):
            a2 = alt and ct % 2 == 1
            tw = t_den if a2 else t_w; tyn = t_rden if a2 else t_yn
            kw_ = 't_den' if a2 else 't_w'; ky_ = 't_rden' if a2 else 't_yn'
            S.op('dve', TT(tw[:, :T], y32[:, ct, :T], t_mu[:, :T], ALU.subtract), reads=arr(20 + 2 * ct, 22 + 2 * ct) + ['t_mu'], writes=[kw_])
            S.op('dve', TT(tw[:, :T], tw[:, :T], t_rstd[:, :T], ALU.mult), reads=[kw_, 't_rstd'], writes=[kw_])
            S.op('dve', TS(tyn[:, :T], tw[:, :T], cvec[:, l, ct, 1:2], cvec[:, l, ct, 2:3], ALU.mult, ALU.add), reads=[kw_, 'cvec'], writes=[ky_])
            S.op('act', ACT(tw[:, :T], tyn[:, :T], AF.Sigmoid), reads=[ky_], writes=[kw_])
            S.op('dve', STT(co[:, ct, :T], tyn[:, :T], cvec[:, l, ct, 3:4], tw[:, :T], ALU.mult, ALU.mult), reads=[ky_, kw_, 'cvec'], writes=[('ar', 16 + ct)])

        def ln_phase(l, T):
            ln_stats(l, T)
            for ct in range(4):
                ln_apply(l, T, ct, True)

        def mix_prompt(l, T, nb, first_main, lastB=None):
            units = [(tb, kvh) for tb in range(nb) for kvh in range(2)]
            nu = len(units)
            v3 = lambda ap: ap.rearrange("p (g q) -> p g q", g=4)

            def A_S(u):
                tb, kvh = units[u]; c0 = tb * 128; par = u % 2; i0 = par * 2; i1 = i0 + 1
                q = qT[:, kvh * 4:(kvh + 1) * 4, c0:c0 + 128]
                S.op('pe', MM(banks[i0], kT[:, l, kvh, c0:c0 + 128], q, True, True), reads=arr(0, 8) + [('kT', l)], writes=[bk(i0)])
                S.op('pe', MM(banks[i1], kT[:, l, kvh, 128 + c0:256 + c0], q, True, True), reads=arr(0, 8) + [('kT', l)], writes=[bk(i1)])

            def A_E(u):
                tb, kvh = units[u]; par = u % 2; i0 = par * 2; i1 = i0 + 1
                P0 = v3(pT[:, par, 0, :]); P1 = v3(pT[:, par, 1, :]); S03 = v3(banks[i0]); S13 = v3(banks[i1])
                bcol = hbf[:, 0:1] if (first_main and tb == 0) else zcol
                S.op('act', [ACT(P0[0:64, :, 0:64], S03[0:64, :, 0:64], AF.Exp, bias=bcol[0:64, :], scale=0.125),
                             ACT(P0[64:128, :, :], S03[64:128, :, :], AF.Exp, bias=bcol[64:128, :], scale=0.125)],
                     reads=[bk(i0), 'hbf', 'zcol'], writes=[('pT', par, 0)])
                S.op('act', [ACT(P1[0:64, :, :], S13[0:64, :, :], AF.Exp, bias=zcol[0:64, :], scale=0.125),
                             ACT(P1[64:128, :, 64:128], S13[64:128, :, 64:128], AF.Exp, bias=zcol[64:128, :], scale=0.125)],
                     reads=[bk(i1), 'zcol'], writes=[('pT', par, 1)])

            def A_PV(u):
                tb, kvh = units[u]; par = u % 2; io = 4 + par
                Ob = banks[io]
                S.op('pe', [MM(Ob, vS[:, l, tb, kvh, :], pT[:, par, 0, :], True, False), MM(Ob, vS[:, l, tb + 1, kvh, :], pT[:, par, 1, :], False, True)],
                     reads=[('pT', par, 0), ('pT', par, 1), ('vS', l, tb), ('vS', l, tb + 1)], writes=[bk(io)])
                attn_epilogue(l, kvh, Ob, io, 512, tb * 128, 128)

            A_S(0)
            if nu > 1:
                A_S(1)
            A_E(0)
            cts = 0; pend = []; stats_done = False
            for u in range(nu):
                if u + 1 < nu:
                    A_E(u + 1)
                A_PV(u)
                if u + 2 < nu:
                    A_S(u + 2)
                if pend:
                    conv_evac(l, T, pend.pop(0))
                    if u == 1 and lastB is not None:
                        lastB()
                    if cts == 4 and not pend and not stats_done:
                        ln_stats(l, T); stats_done = True
                if (u < 6 and u % 2 == 0 or u == 5 or nu <= 4) and cts < 4:
                    conv_mm(l, T, False, cts); pend.append(cts); cts += 1
            while cts < 4 or pend:
                if pend:
                    conv_evac(l, T, pend.pop(0))
                if cts < 4:
                    conv_mm(l, T, False, cts); pend.append(cts); cts += 1
            if not stats_done:
                ln_stats(l, T)
            for ct in range(4):
                ln_apply(l, T, ct, True)

        def sample_cache_prep(l):
            ckf = stg[:, 0, 0:512].rearrange("p (s c) -> p s c", s=4); cvf = stg[:, 1, 0:512].rearrange("p (s c) -> p s c", s=4)
            S.op('pool', DMA(ckf, ck_d[l, :, :, :].rearrange("s k c -> k s c")), writes=['stg0'], dma=True)
            S.op('pool', DMA(cvf, cv_d[l, :, :, :].rearrange("s k c -> k s c")), writes=['stg1'], dma=True)
            S.op('dve', CP(ckb[:], ckf), reads=['stg0'], writes=['ckb'])
            S.op('act', ACPY(vc[:, :, :, 0:64], stg[:, 1, 0:512].rearrange("p (s k d) -> p s k d", s=4, k=2)), reads=['stg1'], writes=['vc'])
            S.op('pe', [TR(PTb[0:64, (s * 2 + kv) * 128:(s * 2 + kv + 1) * 128], ckb[:, s, kv * 64:(kv + 1) * 64], identb[:]) for s in range(4) for kv in range(2)],
                 reads=['ckb', 'identb'], writes=[bk(6), bk(7)])
            S.op('dve', CP(kTc[:], PTb[0:64, 0:1024].rearrange("p (s k t) -> p s k t", s=4, k=2)), reads=[bk(6), bk(7)], writes=['kTc'])
            stf = stg[0:CST, :, :].rearrange("p a (b c) -> p (a b) c", b=2)
            S.op('pool', DMA(stf, stc_d[l, :, :, :].rearrange("s t c -> t s c")), writes=['stg0', 'stg1'], dma=True)
            S.op('pe', [TR(PT[:, (ct * 4 + s) * 32:(ct * 4 + s) * 32 + CST], stf[:, s, ct * 128:(ct + 1) * 128], identf[0:CST, 0:CST]) for ct in range(4) for s in range(4)],
                 reads=['stg0', 'stg1', 'identf'], writes=[bk(6)])
            S.op('dve', CP(usx[:, :, :, 2:32], PT[:, 0:512].rearrange("p (c s t) -> p c s t", c=4, s=4)[:, :, :, 0:CST]), reads=[bk(6)], writes=['usx'])

        def attn_sample(l):
            for kvh in range(2):
                Sc = banks[0]; Sn = banks[1]; Ob = banks[4 + kvh]
                fns = [MM(Sc[:, s * 64:(s + 1) * 64], kTc[:, s, kvh, :], qT[:, kvh * 4:(kvh + 1) * 4, s * 16:(s + 1) * 16], True, True) for s in range(4)]
                S.op('pe', fns, reads=arr(0, 8) + ['kTc'], writes=[bk(0)])
                S.op('pe', MM(Sn[0:64, 0:256], kT[:, l, kvh, 128:192], qT[:, kvh * 4:(kvh + 1) * 4, 0:64], True, True),
                     reads=arr(0, 8) + [('kT', l)], writes=[bk(1)])
                S.op('act', ACT(pTc[:, :], Sc[:, 0:256], AF.Exp, bias=zcol[:], scale=0.125), reads=[bk(0), 'zcol'], writes=['pTc'])
                S.op('act', ACT(pTn[:, :], Sn[0:64, 0:256], AF.Exp, bias=zcol[0:64, :], scale=0.125), reads=[bk(1), 'zcol'], writes=['pTn'])
                pn3 = pTn[:, :].rearrange("p (g q) -> p g q", g=4)
                S.op('dve', TT(pn3, pn3, mskb[:, :].unsqueeze(1).to_broadcast([64, 4, 64]), ALU.mult), reads=['pTn', 'mskb'], writes=['pTn'])
                fns = []
                for s in range(4):
                    fns.append(MM(Ob[:, s * 64:(s + 1) * 64], vc[:, s, kvh, :], pTc[:, s * 64:(s + 1) * 64], True, False))
                    fns.append(MM(Ob[:, s * 64:(s + 1) * 64], vS[0:64, l, 1, kvh, :], pn3[:, :, s * 16:(s + 1) * 16], False, True))
                S.op('pe', fns, reads=['pTc', 'pTn', 'vc', ('vS', l, 1)], writes=[bk(4 + kvh)])
                attn_epilogue(l, kvh, Ob, 4 + kvh, 256, 0, 16, sample=True)

        def wout_phase(l, T):
            wa_, ka_ = wnext(); wc, kc_ = wnext()

            def attn_part(m):
                pb = banks[m % 6]
                S.op('pe', [MM(pb[:, :T], wa_[:, j, m * 128:(m + 1) * 128], OT2[:, j, :T], j == 0, False) for j in range(4)],
                     reads=arr(8, 12) + [ka_], writes=[bk(m % 6)])
            for m in range(6):
                attn_part(m)
            for m in range(8):
                pb = banks[m % 6]
                S.op('pe', [MM(pb[:, :T], wc[:, ct, m * 128:(m + 1) * 128], co[:, ct, :T], False, ct == 3) for ct in range(4)],
                     reads=arr(16, 20) + [kc_], writes=[bk(m % 6)])
                S.op('dve', TT(xT[:, m, :T], xT[:, m, :T], pb[:, :T], ALU.add), reads=[bk(m % 6), ('xT', m)], writes=[('xT', m)])
                if m + 6 < 8:
                    attn_part(m + 6)
                if m >= 1:
                    norm_stats_k(m - 1, T)
            wdone(2)
            norm_stats_k(7, T)
            norm_finish(l, g2T, T)

        def ffn_phase(l, T, pool_sq=True):
            for j in range(8):
                w, kw = wnext()
                for mm in range(4):
                    m = j * 4 + mm
                    ib = m % 4
                    pb = banks[ib]
                    fm = [MM(pb[:, :T], w[:, k, mm * 128:(mm + 1) * 128], hT[:, k, :T], k == 0, k == 7) for k in range(8)]
                    if m == 0:
                        for k in range(8):
                            S.op('pe', fm[k], reads=[('hT', k), kw], writes=[bk(ib)])
                    else:
                        S.op('pe', fm, reads=hTk + [kw], writes=[bk(ib)])
                    if m % 2 == 0:
                        S.op('act', ACT(t_r[:, 0, :T], pb[:, :T], AF.Relu), reads=[bk(ib)], writes=[('t_r', 0)])
                        S.op('dve', TT(ar[:, m, :T], t_r[:, 0, :T], t_r[:, 0, :T], ALU.mult), reads=[('t_r', 0)], writes=[('ar', m)])
                    else:
                        S.op('dve', TS(t_r[:, 1, :T], pb[:, :T], 0.0, None, ALU.max), reads=[bk(ib)], writes=[('t_r', 1)])
                        S.op('pool' if pool_sq else 'dve', TT(ar[:, m, :T], t_r[:, 1, :T], t_r[:, 1, :T], ALU.mult), reads=[('t_r', 1)], writes=[('ar', m)])
                wdone(1)
            for ch in range(2):
                for kc in range(4):
                    w, kw = wnext()
                    fns = []
                    b0 = 4 if ch == 0 else 0
                    for mm in range(4):
                        for kk in range(8):
                            fns.append(MM(banks[b0 + mm][:, :T], w[:, kk, mm * 128:(mm + 1) * 128], ar[:, kc * 8 + kk, :T],
                                          kc == 0 and kk == 0, kc == 3 and kk == 7))
                    S.op('pe', fns, reads=arr(kc * 8, kc * 8 + 8) + [kw], writes=[bk(b0 + i) for i in range(4)])
                    wdone(1)
                for mm in range(4):
                    m = ch * 4 + mm
                    S.op('dve', TT(xT[:, m, :T], xT[:, m, :T], banks[b0 + mm][:, :T], ALU.add), reads=[bk(b0 + mm), ('xT', m)], writes=[('xT', m)])

        def tail_out(l, sample):
            nt = 64 if sample else 32
            S.op('pe', [TR(PT[0:nt, ct * 128:(ct + 1) * 128], u32[:, ct, 0:nt], identf[:]) for ct in range(4)], reads=['u32', 'identf'], writes=[bk(6)])
            S.op('act', ACPY(ostg[0:nt, :], PT[0:nt, 0:512]), reads=[bk(6)], writes=['ostg'])
            if sample:
                for s in range(4):
                    S.op('pool', DMA(sc_d[l, s, 14:30, :], ostg[s * 16:(s + 1) * 16, :]), reads=['ostg'], dma=True)
            else:
                S.op('pool', DMA(pc_d[l, :, :], ostg[2:32, :]), reads=['ostg'], dma=True)

        def hist_shift(l, T, nb, halo):
            S.op('pool', CP(kT[:, l, :, 0:128], kT[:, l, :, T:T + 128]), reads=[('kT', l)], writes=[('kT', l)])
            S.op('pool', CP(vS[:, l, 0, :, 0:64], vS[:, l, nb, :, 0:64]), reads=[('vS', l, nb)], writes=[('vS', l, 0)])
            if halo:
                S.op('pool', TS(uX[:, l, :, 0:32], uX[:, l, :, T:T + 32], hbf[:, 1:2], None, ALU.mult), reads=[('uX', l, c) for c in range(4)] + ['hbf'],
                     writes=[('uX', l, c) for c in range(4)])
            else:
                S.op('pool', CP(uX[:, l, :, 0:32], uX[:, l, :, T:T + 32]), reads=[('uX', l, c) for c in range(4)], writes=[('uX', l, c) for c in range(4)])

        pref = {}
        ldg = dg[:, :, :, :].rearrange("p a b c -> p (a b c)")[:, 0:6144].bitcast(F32).rearrange("p (s c) -> p s c", s=3)
        LDK = [('dg', 0), ('dg', 1)]

        ld3 = sqq2[:, :, :].rearrange("p a b -> p (a b)")[:, 0:1024]
        LD3K = [('sqq', 0), ('sqq', 1)]

        def ldslot(tb):
            if tb % 4 == 3:
                return ld3, LD3K
            return ldg[:, tb % 4, :], LDK

        def prefetch_tile(src, nb, bs, nmax=4):
            for tb in range(min(nb, nmax)):
                ap_, keys_ = ldslot(tb)
                S.op('pool', DMA(ap_[:bs, :], src[tb * bs:(tb + 1) * bs, :]), writes=keys_, dma=True)
                pref[tb] = True

        def load_tile(src, T, nb, bs):
            for tb in range(nb):
                s = tb % 2
                c0 = tb * bs
                PX = P67 if s == 0 else P45; bx = [bk(6), bk(7)] if s == 0 else [bk(4), bk(5)]
                ap_, keys_ = ldslot(tb)
                if not pref.pop(tb, False):
                    S.op('pool', DMA(ap_[:bs, :], src[c0:c0 + bs, :]), writes=keys_, dma=True)
                S.op('pe', [TR(PX[:, f * 128:f * 128 + bs], ap_[:bs, f * 128:(f + 1) * 128], identf[:bs, :bs]) for f in range(8)],
                     reads=keys_ + ['identf'], writes=bx)
                S.op('act' if s == 0 else 'dve', (ACPY if s == 0 else CP)(xT[:, :, c0:c0 + bs], PX[:, :].rearrange("p (f t) -> p f t", f=8)[:, :, :bs]), reads=bx,
                     writes=[('xT', k) for k in range(8)])

        def store_tile(dst, T, nb, bs):
            for tb in range(nb):
                s = tb % 2
                c0 = tb * bs
                PX = P67 if s == 0 else P45; bx = [bk(6), bk(7)] if s == 0 else [bk(4), bk(5)]
                S.op('pe', [TR(PX[:bs, f * 128:(f + 1) * 128], xT[:, f, c0:c0 + bs], identf[:]) for f in range(8)],
                     reads=[('xT', k) for k in range(8)] + ['identf'], writes=bx)
                S.op('act' if s == 0 else 'dve', (ACPY if s == 0 else CP)(stg[:bs, s, :], PX[:bs, :]), reads=bx, writes=['stg%d' % s])
                S.op('pool', DMA(dst[c0:c0 + bs, :], stg[:bs, s, :]), reads=['stg%d' % s], dma=True)

        def run_tile(src, dst, T, nb, bs, sample, halo, first_main, last_main, blk0, nxt=None):
            load_tile(src, T, nb, bs)
            for l in range(2):
                want = sample or last_main
                if sample:
                    sample_cache_prep(l)
                norm_phase(l, g1T, T)
                if halo and l == 1 and nxt is not None:
                    prefetch_tile(*nxt, nmax=3)
                lastB = front_phase(l, T, nb, bs, sample, blk0, want, (64 if sample else 32) if want else 0,
                                    defer_last=(not sample and not halo and nb == 4), build_dg=not (halo and l == 1))
                if dbg == 'glu':
                    raise StopIteration
                if halo and l == 1:
                    hist_shift(l, T, nb, halo)
                    break
                if sample:
                    attn_sample(l)
                    for ct in range(4):
                        conv_mm(l, T, True, ct)
                        conv_evac(l, T, ct)
                    ln_phase(l, T)
                else:
                    mix_prompt(l, T, nb, first_main, lastB)
                if dbg == 'ln':
                    raise StopIteration
                if want:
                    tail_out(l, sample)
                wout_phase(l, T)
                if dbg == 'wout':
                    raise StopIteration
                if l == 1 and nxt is not None:
                    prefetch_tile(*nxt)
                ffn_phase(l, T, pool_sq=not (sample or (halo and l == 0) or (first_main and l == 1)))
                if dbg == 'ffn':
                    raise StopIteration
                if not sample:
                    hist_shift(l, T, nb, halo)
            if dst is not None:
                store_tile(dst, T, nb, bs)

        try:
            run_tile(x_d[0:HALO, :], None, HALO, 2, 128, False, True, False, False, 0,
                     (x_d[HALO:HALO + 512, :], 4, 128) if n_main_tiles > 0 else None)
            for t in range(n_main_tiles):
                if t + 1 < n_main_tiles:
                    nxt = (x_d[HALO + (t + 1) * 512:HALO + (t + 2) * 512, :], 4, 128)
                elif do_sample:
                    nxt = (xs_d, 1, 64)
                else:
                    nxt = None
                run_tile(x_d[HALO + t * 512:HALO + (t + 1) * 512, :], y_d[t * 512:(t + 1) * 512, :], 512, 4, 128, False, False,
                         t == 0, t == n_main_tiles - 1, 2 + t * 4, nxt)
            if do_sample:
                run_tile(xs_d, ys_d, 64, 1, 64, True, False, False, False, 0)
        except StopIteration:
            pass

        block = es.enter_context(nc.Block())

        def emit(e, name):
            for waits, fns, sem, inc in S.streams[name]:
                for s_, v_ in waits:
                    e.wait_ge(sems[s_], v_)
                ins = None
                for f in fns:
                    ins = f(e)
                ins.then_inc(sems[sem], inc)
            if name in ('sp', 'pool'):
                for i in range(NDQ):
                    sn = 'd%s%d' % (name, i)
                    if S.cnt.get(sn, 0):
                        e.wait_ge(sems[sn], S.cnt[sn])

        @block.tensor
        def _(e):
            emit(e, 'pe')

        @block.scalar
        def _(e):
            emit(e, 'act')

        @block.vector
        def _(e):
            emit(e, 'dve')

        @block.gpsimd
        def _(e):
            emit(e, 'pool')

        @block.sync
        def _(e):
            emit(e, 'sp')
    return nc


def _rope_tab(pos):
    half = 8
    inv = np.power(np.float32(THETA), -np.arange(half, dtype=np.float32) * np.float32(2.0) / np.float32(16)).astype(np.float32)
    ang = (pos.astype(np.float32)[:, None] * inv[None, :]).astype(np.float32)
    return np.concatenate([np.cos(ang.astype(np.float64)), np.sin(ang.astype(np.float64))], axis=1).astype(np.float32)


def make_in_maps(x_prompt, x_sample, cache_k, cache_v, state_conv, norm1_g, w_in, q_norm_g, k_norm_g,
                 attn_sinks, conv_w, conv_b, conv_ln_g, conv_ln_b, beta_attn, beta_conv, w_out, norm2_g, w_up, w_down):
    f = lambda a: np.ascontiguousarray(np.asarray(a, dtype=np.float32))
    x_prompt = f(x_prompt); x_sample = f(x_sample)
    shared = {
        "w_in": f(w_in), "w_out": f(w_out), "w_up": f(w_up), "w_down": f(w_down),
        "ident": np.eye(128, dtype=np.float32),
        "g1T": f(np.asarray(norm1_g).reshape(2, 8, 128).transpose(2, 0, 1)),
        "g2T": f(np.asarray(norm2_g).reshape(2, 8, 128).transpose(2, 0, 1)),
        "g10": f(np.broadcast_to(np.concatenate([np.tile(np.asarray(q_norm_g), (1, 8)), np.tile(np.asarray(k_norm_g), (1, 2))], axis=1)[None], (128, 2, 640))),
        "snk": f(np.broadcast_to(np.asarray(attn_sinks)[None], (128, 2, 8))),
        "cwT": f(np.asarray(conv_w).reshape(2, CW, 4, 128).transpose(3, 0, 2, 1)),
        "cvec": f(np.stack([np.asarray(a).reshape(2, 4, 128).transpose(2, 0, 1) for a in (conv_b, conv_ln_g, conv_ln_b, beta_conv)], axis=-1)),
        "ba": f(np.asarray(beta_attn).reshape(2, 8, 64).transpose(2, 0, 1)),
        "css": f(np.tile(_rope_tab(PAST + np.arange(DT)), (4, 1))),
        "msk": f(np.kron(np.eye(4, dtype=np.float32), np.ones((16, 16), np.float32))),
    }
    maps = []
    for c in range(NCORE):
        b, half = c // 2, c % 2
        st = half * MAIN
        xc = np.zeros((NTOK, D), np.float32)
        xc[HALO:] = x_prompt[b, st:st + MAIN]
        if half:
            xc[:HALO] = x_prompt[b, st - HALO:st]
        pos = np.arange(st - HALO, st + MAIN)
        m = dict(shared)
        m["x"] = xc
        m["xs"] = f(x_sample[4 * c:4 * c + 4].reshape(64, D))
        m["ck"] = f(np.asarray(cache_k)[:, 4 * c:4 * c + 4].reshape(2, 4, 128, 128))
        m["cv"] = f(np.asarray(cache_v)[:, 4 * c:4 * c + 4].reshape(2, 4, 128, 128))
        m["stc"] = f(np.asarray(state_conv)[:, 4 * c:4 * c + 4])
        m["csn"] = f(_rope_tab(pos).reshape(NTOK // 128, 128, 16).transpose(1, 0, 2))
        hb = np.zeros((128, 2), np.float32)
        hb[:, 0] = 0.0 if half else -30000.0
        hb[:, 1] = 1.0 if half else 0.0
        m["hbf"] = hb
        maps.append(m)
    return maps


_NC_CACHE = {}


def kernel(**inputs):
    maps = make_in_maps(**inputs)
    if 'nc' not in _NC_CACHE:
        _NC_CACHE['nc'] = build_program()
    nc = _NC_CACHE['nc']
    res = run_bass_kernel_spmd(nc, maps, core_ids=list(range(NCORE))).results
    y = np.zeros((NB, SEQ, D), np.float32)
    ys = np.zeros((DB, DT, D), np.float32)
    pk = np.zeros((2, NB, 128, 2, 64), np.float32); pv = np.zeros_like(pk); pc = np.zeros((2, NB, CST, CCH), np.float32)
    sk = np.zeros((2, DB, 128, 2, 64), np.float32); sv = np.zeros_like(sk); sc = np.zeros((2, DB, CST, CCH), np.float32)
    for c in range(NCORE):
        b, half = c // 2, c % 2
        r = res[c]
        y[b, half * MAIN:(half + 1) * MAIN] = r["y"]
        ys[4 * c:4 * c + 4] = r["ys"].reshape(4, DT, D)
        sk[:, 4 * c:4 * c + 4] = r["sk"].reshape(2, 4, 128, 2, 64)
        sv[:, 4 * c:4 * c + 4] = r["sv"].reshape(2, 4, 128, 2, 64)
        sc[:, 4 * c:4 * c + 4] = r["sc"]
        if half:
            pk[:, b] = r["pk"].reshape(2, 128, 2, 64)
            pv[:, b] = r["pv"].reshape(2, 128, 2, 64)
            pc[:, b] = r["pc"]
    return (y, ys, pk, pv, pc, sk, sv, sc)
```

```python
import numpy as np
from contextlib import ExitStack
import concourse.bass as bass
import concourse.mybir as mybir
from concourse.bass_utils import run_bass_kernel_spmd

F32 = mybir.dt.float32
BF16 = mybir.dt.bfloat16
AF = mybir.ActivationFunctionType
ALU = mybir.AluOpType
AX = mybir.AxisListType

D = 1024; SEQ = 8192; NB = 4; DEPTH = 2; DB = 32; DT = 16; PAST = 2048
HD = 64; NH = 8; NKV = 2; CCH = 512; CW = 31; CST = 30; DFF = 4096
INC = 1792; EPS = 1e-6; THETA = 500000.0
NCORE = 8; MAIN = 4096; HALO = 256; NTOK = MAIN + HALO
NRING = 6; NDQ = 8
import os
USE_LN = int(os.environ.get('K_USE_LN', '1'))
ENGS = ('pe', 'act', 'dve', 'pool', 'sp')


class Sched:
    def __init__(self):
        self.streams = {e: [] for e in ENGS}
        self.cnt = {}
        self.res = {}
        self.known = {e: {} for e in ENGS}
        self.rr = {'sp': 0, 'pool': 0}

    def _need(self, eng, ev, waits):
        if ev is None:
            return
        sem, val = ev
        if eng == 'pe' and sem == 'pe':
            return
        if self.known[eng].get(sem, 0) >= val:
            return
        if waits.get(sem, 0) < val:
            waits[sem] = val

    def op(self, eng, fns, reads=(), writes=(), dma=False):
        if not isinstance(fns, (list, tuple)):
            fns = [fns]
        waits = {}
        for r in reads:
            st = self.res.get(r)
            if st:
                self._need(eng, st[0], waits)
        for w in writes:
            st = self.res.get(w)
            if st:
                self._need(eng, st[0], waits)
                for sem, val in st[1].items():
                    self._need(eng, (sem, val), waits)
        if dma:
            i = self.rr[eng]
            self.rr[eng] = (i + 1) % NDQ
            sem = 'd%s%d' % (eng, i)
            prev = self.cnt.get(sem, 0)
            if prev:
                self._need(eng, (sem, prev), waits)
            val = prev + 16
            inc = 16
        else:
            sem = eng
            val = self.cnt.get(sem, 0) + 1
            inc = 1
        self.cnt[sem] = val
        for s, v in waits.items():
            self.known[eng][s] = v
        for r in reads:
            st = self.res.setdefault(r, [None, {}])
            st[1][sem] = val
        for w in writes:
            self.res[w] = [(sem, val), {}]
        self.streams[eng].append((sorted(waits.items()), list(fns), sem, inc))


def MM(out, lhsT, rhs, start, stop):
    return lambda e: e.matmul(out, lhsT=lhsT, rhs=rhs, start=start, stop=stop)


def TR(out, in_, ident):
    return lambda e: e.transpose(out=out, in_=in_, identity=ident)


def ACT(out, in_, func, bias=None, scale=None):
    kw = {}
    if bias is not None:
        kw['bias'] = bias
    if scale is not None:
        kw['scale'] = scale
    return lambda e: e.activation(out=out, in_=in_, func=func, **kw)


def TT(out, in0, in1, op):
    return lambda e: e.tensor_tensor(out=out, in0=in0, in1=in1, op=op)


def STT(out, in0, scalar, in1, op0, op1):
    return lambda e: e.scalar_tensor_tensor(out=out, in0=in0, scalar=scalar, in1=in1, op0=op0, op1=op1)


def TS(out, in0, s1, s2, op0, op1=None):
    if op1 is None:
        return lambda e: e.tensor_scalar(out=out, in0=in0, scalar1=s1, scalar2=None, op0=op0)
    return lambda e: e.tensor_scalar(out=out, in0=in0, scalar1=s1, scalar2=s2, op0=op0, op1=op1)


def CP(out, in_):
    return lambda e: e.tensor_copy(out=out, in_=in_)


def ACPY(out, in_):
    return lambda e: e.copy(out=out, in_=in_)


def RCP(out, in_):
    return lambda e: e.reciprocal(out=out, in_=in_)


def MS(ap, v):
    return lambda e: e.memset(ap, v)


def DMA(out, in_):
    return lambda e: e.dma_start(out=out, in_=in_)


def build_program(n_main_tiles=8, do_sample=True, dbg=None):
    nc = bass.Bass("TRN2", target_bir_lowering=False)
    S = Sched()

    def din(name, shape, dt=F32):
        return nc.dram_tensor(name, list(shape), dt, kind="ExternalInput").ap()

    def dout(name, shape):
        return nc.dram_tensor(name, list(shape), F32, kind="ExternalOutput").ap()

    x_d = din("x", [NTOK, D]); xs_d = din("xs", [64, D])
    ck_d = din("ck", [2, 4, 128, 128]); cv_d = din("cv", [2, 4, 128, 128]); stc_d = din("stc", [2, 4, CST, CCH])
    wi_d = din("w_in", [2, D, INC]); wo_d = din("w_out", [2, D, D]); wu_d = din("w_up", [2, D, DFF]); wd_d = din("w_down", [2, DFF, D])
    ident_d = din("ident", [128, 128]); g1_d = din("g1T", [128, 2, 8]); g2_d = din("g2T", [128, 2, 8])
    g10_d = din("g10", [128, 2, 640]); snk_d = din("snk", [128, 2, 8]); cw_d = din("cwT", [128, 2, 4, CW])
    cvec_d = din("cvec", [128, 2, 4, 4]); ba_d = din("ba", [64, 2, 8]); csn_d = din("csn", [128, NTOK // 128, 16])
    css_d = din("css", [64, 16]); hbf_d = din("hbf", [128, 2]); msk_d = din("msk", [64, 64])
    y_d = dout("y", [MAIN, D]); ys_d = dout("ys", [64, D])
    pk_d = dout("pk", [2, 128, 128]); pv_d = dout("pv", [2, 128, 128]); pc_d = dout("pc", [2, CST, CCH])
    sk_d = dout("sk", [2, 4, 128, 128]); sv_d = dout("sv", [2, 4, 128, 128]); sc_d = dout("sc", [2, 4, CST, CCH])
    wib = nc.dram_tensor("wib", [2, D, INC], BF16, kind="Internal").ap()
    wob = nc.dram_tensor("wob", [2, D, D], BF16, kind="Internal").ap()
    wub = nc.dram_tensor("wub", [2, D, DFF], BF16, kind="Internal").ap()
    wdb = nc.dram_tensor("wdb", [2, DFF, D], BF16, kind="Internal").ap()

    es = ExitStack()
    with es:
        def sb(name, shape, dt=F32):
            return es.enter_context(nc.sbuf_tensor(name, list(shape), dt))

        def ps(name):
            return es.enter_context(nc.psum_tensor(name, [128, 1024], F32))

        xT = sb("xT", [128, 8, 512]); hT = sb("hT", [128, 8, 512], BF16); ar = sb("ar", [128, 32, 512], BF16)
        kT = sb("kT", [64, 2, 2, 640], BF16); vS = sb("vS", [128, 2, 5, 2, 128], BF16); uX = sb("uX", [128, 2, 4, 544], BF16)
        ring = sb("ring", [128, NRING, 4096], BF16); stg = sb("stg", [128, 2, 1024]); dg = sb("dg", [128, 2, CW, 128], BF16)
        pT = sb("pT", [128, 2, 2, 512], BF16)
        t_rs = sb("t_rs", [128, 512]); t_rstd = sb("t_rstd", [128, 512]); t_sg = sb("t_sg", [128, 512])
        t_mu = sb("t_mu", [128, 512]); t_a = sb("t_a", [128, 512]); t_w = sb("t_w", [128, 512]); t_yn = sb("t_yn", [128, 512])
        t_den = sb("t_den", [128, 512]); t_rden = sb("t_rden", [128, 512])
        t_r = sb("t_r", [128, 2, 512], BF16)
        sqq2 = sb("sqq", [128, 2, 640]); qkn2 = sb("qkn", [128, 2, 640]); qkb2 = sb("qkb", [128, 2, 640], BF16)
        ss102 = sb("ss10", [128, 2, 10]); rs102 = sb("rs10", [128, 2, 10]); rq102 = sb("rq10", [128, 2, 10])
        rp2 = sb("rp", [128, 2, 4, 80])
        v32 = sb("v32", [128, 128]); u32 = sb("u32", [128, 4, 64]); ostg = sb("ostg", [64, 512])
        identf = sb("identf", [128, 128]); identb = sb("identb", [128, 128], BF16)
        onesf = sb("onesf", [128, 128]); onesb = sb("onesb", [128, 128], BF16)
        g1T = sb("g1T_s", [128, 2, 8]); g2T = sb("g2T_s", [128, 2, 8]); g10 = sb("g10_s", [128, 2, 640])
        snk = sb("snk_s", [128, 2, 8]); esink = sb("esink", [128, 2, 8]); cwT = sb("cwT_s", [128, 2, 4, CW])
        cvec = sb("cvec_s", [128, 2, 4, 4]); ba = sb("ba_s", [64, 2, 8]); csn = sb("csn_s", [128, NTOK // 128, 16])
        css = sb("css_s", [64, 16]); hbf = sb("hbf_s", [128, 2]); mskf = sb("mskf", [64, 64]); mskb = sb("mskb", [64, 64], BF16)
        zcol = sb("zcol", [128, 1]); ecol = sb("ecol", [128, 1])
        kTc = sb("kTc", [64, 4, 2, 128], BF16); vc = sb("vc", [128, 4, 2, 128], BF16); usx = sb("usx", [128, 4, 4, 48], BF16)
        ckb = sb("ckb", [128, 4, 128], BF16); pTc = sb("pTc", [128, 256], BF16); pTn = sb("pTn", [64, 256], BF16)
        P01 = ps("P01"); P23 = ps("P23"); P45 = ps("P45"); P67 = ps("P67")
        banks = [P01[:, 0:512], P01[:, 512:1024], P23[:, 0:512], P23[:, 512:1024],
                 P45[:, 0:512], P45[:, 512:1024], P67[:, 0:512], P67[:, 512:1024]]
        PQ = P45; PT = P67
        PTb = P67[:, :].bitcast(BF16)

        def bk(i):
            return ('ps', i)

        qT = ar[0:64, 0:8, :]; co = ar[:, 16:20, :]

        def f32view(lo, n):
            return ar[:, lo:lo + 2 * n, :].rearrange("p a b -> p (a b)").bitcast(F32).rearrange("p (c t) -> p c t", c=n)
        y32 = f32view(20, 4)
        def arr(lo, hi):
            return [('ar', i) for i in range(lo, hi)]

        sem_names = list(ENGS[:4]) + ['dsp%d' % i for i in range(NDQ)] + ['dpool%d' % i for i in range(NDQ)]
        sems = {n: es.enter_context(nc.semaphore("s_" + n)) for n in sem_names}

        def load_const(dst, src, key):
            S.op('sp', DMA(dst, src), writes=[key], dma=True)
        load_const(identf[:], ident_d[:, :], 'identf'); load_const(g1T[:], g1_d[:, :, :], 'g1T'); load_const(g2T[:], g2_d[:, :, :], 'g2T')
        load_const(g10[:], g10_d[:, :, :], 'g10'); load_const(snk[:], snk_d[:, :, :], 'snk'); load_const(cwT[:], cw_d[:, :, :, :], 'cwT')
        load_const(cvec[:], cvec_d[:, :, :, :], 'cvec'); load_const(ba[:], ba_d[:, :, :], 'ba'); load_const(csn[:], csn_d[:, :, :], 'csn')
        load_const(css[:], css_d[:, :], 'css'); load_const(hbf[:], hbf_d[:, :], 'hbf'); load_const(mskf[:], msk_d[:, :], 'mskf')
        if do_sample:
            S.op('pool', DMA(sk_d[:, :, 0:112, :], ck_d[:, :, 16:128, :]), dma=True)
            S.op('pool', DMA(sv_d[:, :, 0:112, :], cv_d[:, :, 16:128, :]), dma=True)
            S.op('pool', DMA(sc_d[:, :, 0:14, :], stc_d[:, :, 16:30, :]), dma=True)
        S.op('dve', CP(identb[:], identf[:]), reads=['identf'], writes=['identb'])
        S.op('dve', CP(mskb[:], mskf[:]), reads=['mskf'], writes=['mskb'])
        S.op('dve', MS(onesf[:], 1.0), writes=['onesf']); S.op('dve', MS(onesb[:], 1.0), writes=['onesb'])
        S.op('dve', MS(zcol[:], 0.0), writes=['zcol']); S.op('dve', MS(ecol[:], EPS), writes=['ecol'])
        S.op('dve', MS(kT[:].rearrange("p a b c -> p (a b c)"), 0.0), writes=[('kT', 0), ('kT', 1)])
        S.op('dve', MS(vS[:].rearrange("p a b c d -> p (a b c d)"), 1.0), writes=[('vS', l, b) for l in range(2) for b in range(5)])
        S.op('dve', MS(vS[:, :, 0, :, 0:64], 0.0), writes=[('vS', l, 0) for l in range(2)])
        S.op('dve', MS(uX[:].rearrange("p a b c -> p (a b c)"), 0.0), writes=[('uX', l, c) for l in range(2) for c in range(4)])
        S.op('dve', MS(pT[:].rearrange("p a b c -> p (a b c)"), 0.0), writes=[('pT', i, j) for i in range(2) for j in range(2)])
        S.op('dve', MS(vc[:].rearrange("p a b c -> p (a b c)"), 1.0), writes=['vc']); S.op('dve', MS(usx[:].rearrange("p a b c -> p (a b c)"), 0.0), writes=['usx'])
        S.op('act', ACT(esink[:], snk[:], AF.Exp), reads=['snk'], writes=['esink'])

        chunks = []

        def add_layer_chunks(l, front_only=False):
            def kview(n):
                return lambda s: ring[:, s, 0:8 * n].rearrange("p (k c) -> p k c", k=8)
            for ci, (c0, c1) in enumerate(((0, 512), (512, 768), (768, 1280), (1280, 1792))):
                chunks.append((wib[l, :, c0:c1].rearrange("(k p) c -> p k c", p=128), kview(c1 - c0), ('wbc', l, ci),
                               wib[l, :, c0:c1], wi_d[l, :, c0:c1]))
            if front_only:
                return
            for ci, r0 in enumerate((0, 512)):
                chunks.append((wob[l, r0:r0 + 512, :].rearrange("(k p) c -> p k c", p=128),
                               lambda s: ring[:, s, 0:4096].rearrange("p (k c) -> p k c", k=4), ('wbc', l, 4 + ci),
                               wob[l, r0:r0 + 512, :], wo_d[l, r0:r0 + 512, :]))
            for j in range(8):
                chunks.append((wub[l, :, j * 512:(j + 1) * 512].rearrange("(k p) c -> p k c", p=128), kview(512), ('wbc', l, 6 + j),
                               wub[l, :, j * 512:(j + 1) * 512], wu_d[l, :, j * 512:(j + 1) * 512]))
            for ch in range(2):
                for kc in range(4):
                    chunks.append((wdb[l, kc * 1024:(kc + 1) * 1024, ch * 512:(ch + 1) * 512].rearrange("(k p) c -> p k c", p=128),
                                   kview(512), ('wbc', l, 14 + ch * 4 + kc),
                                   wdb[l, kc * 1024:(kc + 1) * 1024, ch * 512:(ch + 1) * 512],
                                   wd_d[l, kc * 1024:(kc + 1) * 1024, ch * 512:(ch + 1) * 512]))
        add_layer_chunks(0); add_layer_chunks(1, front_only=True)
        for _t in range(n_main_tiles):
            add_layer_chunks(0); add_layer_chunks(1)
        if do_sample:
            add_layer_chunks(0); add_layer_chunks(1)
        cast_done = set()
        wst = {'issued': 0, 'next': 0, 'released': 0}

        def wpump():
            while wst['issued'] < len(chunks) and wst['issued'] - wst['released'] < NRING:
                j = wst['issued']
                src, vf, key, breg, freg = chunks[j]
                s = j % NRING
                if key not in cast_done:
                    cast_done.add(key)
                    S.op('pool', DMA(breg, freg), writes=[key], dma=True)
                S.op('sp', DMA(vf(s), src), reads=[key], writes=[('ring', s)], dma=True)
                wst['issued'] += 1

        def wnext():
            i = wst['next']
            wpump()
            assert wst['issued'] > i, "weight ring: too many chunks held"
            wst['next'] += 1
            s = i % NRING
            return chunks[i][1](s), ('ring', s)

        def wdone(n=1):
            wst['released'] += n
            assert wst['released'] <= wst['next']
            wpump()

        OT2 = ar[:, 8:12, :]
        sq = ar[:, 0:8, :]
        SSB = 7

        def norm_stats_k(k, T):
            S.op('act', ACT(sq[:, k, :T], xT[:, k, :T], AF.Square), reads=[('xT', k)], writes=[('ar', k)])
            S.op('pe', MM(banks[SSB][:, :T], onesb[:], sq[:, k, :T], k == 0, k == 7), reads=[('ar', k), 'onesb'], writes=[bk(SSB)])

        def norm_finish(l, gT, T):
            if USE_LN:
                S.op('act', ACT(t_rs[:, :T], banks[SSB][:, :T], AF.Ln, bias=ecol[:], scale=1.0 / D), reads=[bk(SSB), 'ecol'], writes=['t_rs'])
                S.op('act', ACT(t_rstd[:, :T], t_rs[:, :T], AF.Exp, bias=zcol[:], scale=-0.5), reads=['t_rs', 'zcol'], writes=['t_rstd'])
            else:
                S.op('act', ACT(t_rs[:, :T], banks[SSB][:, :T], AF.Sqrt, bias=ecol[:], scale=1.0 / D), reads=[bk(SSB), 'ecol'], writes=['t_rs'])
                S.op('dve', RCP(t_rstd[:, :T], t_rs[:, :T]), reads=['t_rs'], writes=['t_rstd'])
            for k in range(8):
                S.op('dve', STT(hT[:, k, :T], xT[:, k, :T], gT[:, l, k:k + 1], t_rstd[:, :T], ALU.mult, ALU.mult),
                     reads=[('xT', k), 't_rstd', 'g1T', 'g2T'], writes=[('hT', k)])

        def norm_phase(l, gT, T):
            for k in range(8):
                if k % 2 == 1:
                    S.op('dve', TT(sq[:, k, :T], xT[:, k, :T], xT[:, k, :T], ALU.mult), reads=[('xT', k)], writes=[('ar', k)])
                    S.op('pe', MM(banks[SSB][:, :T], onesb[:], sq[:, k, :T], k == 0, k == 7), reads=[('ar', k), 'onesb'], writes=[bk(SSB)])
                else:
                    norm_stats_k(k, T)
            norm_finish(l, gT, T)

        hTk = [('hT', k) for k in range(8)]

        def front_phase(l, T, nb, bs, sample, blk0, want_kv_out, tail, defer_last=False):
            wq, kq = wnext(); wkv, kkv = wnext(); wa, ka = wnext(); wb_, kb = wnext()
            PQs = (P01, P23)

            def qkv_A(tb):
                c0 = tb * bs; pr = tb % 2
                PQ = PQs[pr]; bq = [bk(2 * pr), bk(2 * pr + 1)]
                sqq = sqq2[:, pr, :]; qkn = qkn2[:, pr, :]; qkb = qkb2[:, pr, :]
                ss10 = ss102[:, pr, :]; rs10 = rs102[:, pr, :]; rq10 = rq102[:, pr, :]
                K = lambda n: (n, pr)
                fq = [MM(PQ[:bs, 0:512], hT[:, k, c0:c0 + bs], wq[:, k, :], k == 0, k == 7) for k in range(8)]
                fkv = [MM(PQ[:bs, 512:768], hT[:, k, c0:c0 + bs], wkv[:, k, :], k == 0, k == 7) for k in range(8)]
                if tb == 0:
                    for k in range(8):
                        S.op('pe', fq[k], reads=[('hT', k), kq], writes=bq)
                    S.op('pe', fkv, reads=hTk + [kkv], writes=bq)
                else:
                    S.op('pe', fq + fkv, reads=hTk + [kq, kkv], writes=bq)
                S.op('act', ACT(sqq[:bs, :], PQ[:bs, 0:640], AF.Square), reads=bq, writes=[K('sqq')])
                S.op('dve', lambda e: e.tensor_reduce(out=ss10[:bs, :], in_=sqq[:bs, :].rearrange("p (h d) -> p h d", h=10), axis=AX.X, op=ALU.add),
                     reads=[K('sqq')], writes=[K('ss10')])
                S.op('act', ACT(rs10[:bs, :], ss10[:bs, :], AF.Ln, bias=ecol[:bs, :], scale=1.0 / HD), reads=[K('ss10'), 'ecol'], writes=[K('rs10')])
                S.op('act', ACT(rq10[:bs, :], rs10[:bs, :], AF.Exp, bias=zcol[:bs, :], scale=-0.5), reads=[K('rs10'), 'zcol'], writes=[K('rq10')])
                q3 = qkn[:bs, :].rearrange("p (h d) -> p h d", h=10)
                S.op('dve', TT(q3, PQ[:bs, 0:640].rearrange("p (h d) -> p h d", h=10), rq10[:bs, :].unsqueeze(2).to_broadcast([bs, 10, 64]), ALU.mult),
                     reads=bq + [K('rq10')], writes=[K('qkn')])
                S.op('act', ACPY(vS[:bs, l, tb + 1, :, 0:64], PQ[:bs, 640:768].rearrange("p (k d) -> p k d", k=2)),
                     reads=bq, writes=[('vS', l, tb + 1)])
                kvout = want_kv_out and (sample or tb == nb - 1)
                if kvout:
                    S.op('act', ACPY(v32[:bs, :], PQ[:bs, 640:768]), reads=bq, writes=['v32'])
                S.op('dve', TT(qkn[:bs, :], qkn[:bs, :], g10[:bs, l, :], ALU.mult), reads=[K('qkn'), 'g10'], writes=[K('qkn')])
                if sample:
                    cs_ = css[:bs, 0:8]; sn_ = css[:bs, 8:16]
                else:
                    cs_ = csn[:bs, blk0 + tb, 0:8]; sn_ = csn[:bs, blk0 + tb, 8:16]
                cosb = cs_.unsqueeze(1).to_broadcast([bs, 10, 8]); sinb = sn_.unsqueeze(1).to_broadcast([bs, 10, 8])
                x1 = q3[:, :, 0:8]; x2 = q3[:, :, 8:16]
                r = [rp2[:bs, pr, i, :].rearrange("p (h d) -> p h d", h=10) for i in range(4)]
                S.op('dve', [TT(r[0], x1, cosb, ALU.mult), TT(r[1], x2, sinb, ALU.mult), TT(r[2], x2, cosb, ALU.mult), TT(r[3], x1, sinb, ALU.mult)],
                     reads=[K('qkn'), 'csn', 'css'], writes=[K('rp')])
                S.op('dve', [TT(x1, r[0], r[1], ALU.subtract), TT(x2, r[2], r[3], ALU.add)], reads=[K('rp')], writes=[K('qkn')])
                S.op('act', ACPY(qkb[:bs, :], qkn[:bs, :]), reads=[K('qkn')], writes=[K('qkb')])
                if kvout:
                    if sample:
                        for s in range(4):
                            S.op('pool', DMA(sk_d[l, s, 112:128, :], qkn[s * 16:(s + 1) * 16, 512:640]), reads=[K('qkn')], dma=True)
                            S.op('pool', DMA(sv_d[l, s, 112:128, :], v32[s * 16:(s + 1) * 16, :]), reads=['v32'], dma=True)
                    else:
                        S.op('pool', DMA(pk_d[l, :, :], qkn[:, 512:640]), reads=[K('qkn')], dma=True)
                        S.op('pool', DMA(pv_d[l, :, :], v32[:, :]), reads=['v32'], dma=True)

            def qkv_B(tb):
                c0 = tb * bs; pr = tb % 2
                qkb = qkb2[:, pr, :]
                S.op('pe', [TR(PTb[0:64, hd * 128:hd * 128 + bs], qkb[:bs, hd * 64:(hd + 1) * 64], identb[:bs, :bs]) for hd in range(10)],
                     reads=[('qkb', pr), 'identb'], writes=[bk(6), bk(7)])
                S.op('dve', CP(qT[:, :, c0:c0 + bs], PTb[0:64, 0:1024].rearrange("p (h t) -> p h t", h=8)[:, :, :bs]),
                     reads=[bk(6), bk(7)], writes=arr(0, 8))
                S.op('dve', CP(kT[:, l, :, 128 + c0:128 + c0 + bs], PTb[0:64, 1024:1280].rearrange("p (h t) -> p h t", h=2)[:, :, :bs]),
                     reads=[bk(6), bk(7)], writes=[('kT', l)])

            def glu_ct(ct):
                ia = 4; ib = 5
                pa = banks[ia]; pb = banks[ib]
                S.op('pe', [MM(pa[:, :T], wa[:, k, ct * 128:(ct + 1) * 128], hT[:, k, :T], k == 0, k == 7) for k in range(8)],
                     reads=hTk + [ka], writes=[bk(ia)])
                S.op('pe', [MM(pb[:, :T], wb_[:, k, ct * 128:(ct + 1) * 128], hT[:, k, :T], k == 0, k == 7) for k in range(8)],
                     reads=hTk + [kb], writes=[bk(ib)])
                S.op('act', ACT(t_sg[:, :T], pb[:, :T], AF.Sigmoid), reads=[bk(ib)], writes=['t_sg'])
                if sample:
                    S.op('dve', TT(usx[:, ct, :, 32:48], pa[:, :64].rearrange("p (s t) -> p s t", s=4),
                                   t_sg[:, :64].rearrange("p (s t) -> p s t", s=4), ALU.mult), reads=[bk(ia), 't_sg'], writes=['usx'])
                else:
                    S.op('dve', TT(uX[:, l, ct, 32:32 + T], pa[:, :T], t_sg[:, :T], ALU.mult), reads=[bk(ia), 't_sg'], writes=[('uX', l, ct)])
                if tail:
                    S.op('dve', TT(u32[:, ct, 0:tail], pa[:, T - tail:T], t_sg[:, T - tail:T], ALU.mult), reads=[bk(ia), 't_sg'], writes=['u32'])

            dg_build(l, 0); dg_build(l, 1)
            for i in range(4):
                if i < nb:
                    qkv_A(i)
                glu_ct(i)
                if i >= 1 and i - 1 < nb:
                    qkv_B(i - 1)
            wdone(4)
            if nb == 4:
                if defer_last:
                    return lambda: qkv_B(3)
                qkv_B(3)
            return None

        def attn_epilogue(l, kvh, Ob, ibk, ncol, qlo, nq, sample=False):
            es_ = esink[64:128, l, kvh * 4:(kvh + 1) * 4]
            if sample:
                v4 = lambda ap: ap.rearrange("p (s g q) -> p s g q", s=4, g=4)
                S.op('dve', TT(v4(t_den[0:64, :ncol]), v4(Ob[64:128, :ncol]), es_.unsqueeze(1).unsqueeze(3).to_broadcast([64, 4, 4, 16]), ALU.add),
                     reads=[bk(ibk), 'esink'], writes=['t_den'])
            else:
                v3 = lambda ap: ap.rearrange("p (g q) -> p g q", g=4)
                S.op('dve', TT(v3(t_den[0:64, :ncol]), v3(Ob[64:128, :ncol]), es_.unsqueeze(2).to_broadcast([64, 4, nq]), ALU.add),
                     reads=[bk(ibk), 'esink'], writes=['t_den'])
            if USE_LN:
                S.op('act', ACT(t_den[0:64, :ncol], t_den[0:64, :ncol], AF.Ln, bias=zcol[0:64, :], scale=1.0), reads=['t_den', 'zcol'], writes=['t_den'])
                S.op('act', ACT(t_rden[0:64, :ncol], t_den[0:64, :ncol], AF.Exp, bias=zcol[0:64, :], scale=-1.0), reads=['t_den', 'zcol'], writes=['t_rden'])
            else:
                S.op('dve', RCP(t_rden[0:64, :ncol], t_den[0:64, :ncol]), reads=['t_den'], writes=['t_rden'])
            for g in range(4):
                h = kvh * 4 + g
                po = (h % 2) * 64
                if sample:
                    src = Ob[0:64, :ncol].rearrange("p (s g q) -> p s g q", s=4, g=4)[:, :, g, :]
                    rd_ = t_rden[0:64, :ncol].rearrange("p (s g q) -> p s g q", s=4, g=4)[:, :, g, :]
                    dst = OT2[po:po + 64, h // 2, 0:64].rearrange("p (s q) -> p s q", s=4)
                else:
                    src = Ob[0:64, g * nq:(g + 1) * nq]; rd_ = t_rden[0:64, g * nq:(g + 1) * nq]; dst = OT2[po:po + 64, h // 2, qlo:qlo + nq]
                S.op('dve', STT(dst, src, ba[:, l, h:h + 1], rd_, ALU.mult, ALU.mult),
                     reads=[bk(ibk), 't_rden', 'ba'], writes=[('ar', 8 + h // 2)])

        y16 = ar[:, 28:32, :]
        ysqb = ar[:, 12:16, :]

        def dg_build(l, ct):
            d = ct % 2
            S.op('pool', TT(dg[:, d, :, :], identb[:].unsqueeze(1).to_broadcast([128, CW, 128]),
                            cwT[:, l, ct, :].unsqueeze(2).to_broadcast([128, CW, 128]), ALU.mult),
                 reads=['identb', 'cwT'], writes=[('dg', d)])

        def conv_mm(l, T, sample, ct):
            d = ct % 2
            Y = banks[6 + d]
            if sample:
                fns = [MM(Y[:, 0:64], dg[:, d, j, :], usx[:, ct, :, 2 + j:2 + j + 16], j == 0, j == CW - 1) for j in range(CW)]
                rd = ['usx']
            else:
                fns = [MM(Y[:, :T], dg[:, d, j, :], uX[:, l, ct, 2 + j:2 + j + T], j == 0, j == CW - 1) for j in range(CW)]
                rd = [('uX', l, ct)]
            S.op('pe', fns, reads=rd + [('dg', d)], writes=[bk(6 + d)])
            if ct + 2 < 4:
                dg_build(l, ct + 2)

        def conv_evac(l, T, ct):
            d = ct % 2
            Y = banks[6 + d]
            S.op('act', ACT(y32[:, ct, :T], Y[:, :T], AF.Identity, bias=cvec[:, l, ct, 0:1]), reads=[bk(6 + d), 'cvec'], writes=arr(20 + 2 * ct, 22 + 2 * ct))
            S.op('act', ACT(ysqb[:, ct, :T], Y[:, :T], AF.Square, bias=cvec[:, l, ct, 0:1]), reads=[bk(6 + d), 'cvec'], writes=[('ar', 12 + ct)])
            S.op('act', ACT(y16[:, ct, :T], Y[:, :T], AF.Identity, bias=cvec[:, l, ct, 0:1]), reads=[bk(6 + d), 'cvec'], writes=[('ar', 28 + ct)])

        def ln_stats(l, T):
            S.op('pe', [MM(banks[6][:, :T], onesb[:], y16[:, ct, :T], ct == 0, ct == 3) for ct in range(4)], reads=arr(28, 32) + ['onesb'], writes=[bk(6)])
            S.op('pe', [MM(banks[7][:, :T], onesb[:], ysqb[:, ct, :T], ct == 0, ct == 3) for ct in range(4)], reads=arr(12, 16) + ['onesb'], writes=[bk(7)])
            S.op('act', ACT(t_mu[:, :T], banks[6][:, :T], AF.Identity, bias=zcol[:], scale=1.0 / CCH), reads=[bk(6), 'zcol'], writes=['t_mu'])
            S.op('dve', TT(t_a[:, :T], t_mu[:, :T], t_mu[:, :T], ALU.mult), reads=['t_mu'], writes=['t_a'])
            S.op('dve', STT(t_a[:, :T], banks[7][:, :T], 1.0 / CCH, t_a[:, :T], ALU.mult, ALU.subtract), reads=[bk(7), 't_a'], writes=['t_a'])
            if USE_LN:
                S.op('act', ACT(t_rs[:, :T], t_a[:, :T], AF.Ln, bias=ecol[:], scale=1.0), reads=['t_a', 'ecol'], writes=['t_rs'])
                S.op('act', ACT(t_rstd[:, :T], t_rs[:, :T], AF.Exp, bias=zcol[:], scale=-0.5), reads=['t_rs', 'zcol'], writes=['t_rstd'])
            else:
                S.op('act', ACT(t_rs[:, :T], t_a[:, :T], AF.Sqrt, bias=ecol[:], scale=1.0), reads=['t_a', 'ecol'], writes=['t_rs'])
                S.op('dve', RCP(t_rstd[:, :T], t_rs[:, :T]), reads=['t_rs'], writes=['t_rstd'])

        def ln_apply(l, T, ct, alt):
            a2 = alt and ct % 2 == 1
            tw = t_den if a2 else t_w; tyn = t_rden if a2 else t_yn
            kw_ = 't_den' if a2 else 't_w'; ky_ = 't_rden' if a2 else 't_yn'
            S.op('dve', TT(tw[:, :T], y32[:, ct, :T], t_mu[:, :T], ALU.subtract), reads=arr(20 + 2 * ct, 22 + 2 * ct) + ['t_mu'], writes=[kw_])
            S.op('dve', TT(tw[:, :T], tw[:, :T], t_rstd[:, :T], ALU.mult), reads=[kw_, 't_rstd'], writes=[kw_])
            S.op('dve', TS(tyn[:, :T], tw[:, :T], cvec[:, l, ct, 1:2], cvec[:, l, ct, 2:3], ALU.mult, ALU.add), reads=[kw_, 'cvec'], writes=[ky_])
            S.op('act', ACT(tw[:, :T], tyn[:, :T], AF.Sigmoid), reads=[ky_], writes=[kw_])
            S.op('dve', STT(co[:, ct, :T], tyn[:, :T], cvec[:, l, ct, 3:4], tw[:, :T], ALU.mult, ALU.mult), reads=[ky_, kw_, 'cvec'], writes=[('ar', 16 + ct)])

        def ln_phase(l, T):
            ln_stats(l, T)
            for ct in range(4):
                ln_apply(l, T, ct, True)

        def mix_prompt(l, T, nb, first_main, lastB=None):
            units = [(tb, kvh) for tb in range(nb) for kvh in range(2)]
            nu = len(units)
            v3 = lambda ap: ap.rearrange("p (g q) -> p g q", g=4)

            def A_S(u):
                tb, kvh = units[u]; c0 = tb * 128; par = u % 2; i0 = par * 2; i1 = i0 + 1
                q = qT[:, kvh * 4:(kvh + 1) * 4, c0:c0 + 128]
                S.op('pe', MM(banks[i0], kT[:, l, kvh, c0:c0 + 128], q, True, True), reads=arr(0, 8) + [('kT', l)], writes=[bk(i0)])
                S.op('pe', MM(banks[i1], kT[:, l, kvh, 128 + c0:256 + c0], q, True, True), reads=arr(0, 8) + [('kT', l)], writes=[bk(i1)])

            def A_E(u):
                tb, kvh = units[u]; par = u % 2; i0 = par * 2; i1 = i0 + 1
                P0 = v3(pT[:, par, 0, :]); P1 = v3(pT[:, par, 1, :]); S03 = v3(banks[i0]); S13 = v3(banks[i1])
                bcol = hbf[:, 0:1] if (first_main and tb == 0) else zcol
                S.op('act', [ACT(P0[0:64, :, 0:64], S03[0:64, :, 0:64], AF.Exp, bias=bcol[0:64, :], scale=0.125),
                             ACT(P0[64:128, :, :], S03[64:128, :, :], AF.Exp, bias=bcol[64:128, :], scale=0.125)],
                     reads=[bk(i0), 'hbf', 'zcol'], writes=[('pT', par, 0)])
                S.op('act', [ACT(P1[0:64, :, :], S13[0:64, :, :], AF.Exp, bias=zcol[0:64, :], scale=0.125),
                             ACT(P1[64:128, :, 64:128], S13[64:128, :, 64:128], AF.Exp, bias=zcol[64:128, :], scale=0.125)],
                     reads=[bk(i1), 'zcol'], writes=[('pT', par, 1)])

            def A_PV(u):
                tb, kvh = units[u]; par = u % 2; io = 4 + par
                Ob = banks[io]
                S.op('pe', [MM(Ob, vS[:, l, tb, kvh, :], pT[:, par, 0, :], True, False), MM(Ob, vS[:, l, tb + 1, kvh, :], pT[:, par, 1, :], False, True)],
                     reads=[('pT', par, 0), ('pT', par, 1), ('vS', l, tb), ('vS', l, tb + 1)], writes=[bk(io)])
                attn_epilogue(l, kvh, Ob, io, 512, tb * 128, 128)

            A_S(0)
            if nu > 1:
                A_S(1)
            A_E(0)
            cts = 0; pend = []; stats_done = False
            for u in range(nu):
                if u + 1 < nu:
                    A_E(u + 1)
                A_PV(u)
                if u + 2 < nu:
                    A_S(u + 2)
                if pend:
                    conv_evac(l, T, pend.pop(0))
                    if u == 1 and lastB is not None:
                        lastB()
                    if cts == 4 and not pend and not stats_done:
                        ln_stats(l, T); stats_done = True
                if (u < 6 and u % 2 == 0 or u == 5 or nu <= 4) and cts < 4:
                    conv_mm(l, T, False, cts); pend.append(cts); cts += 1
            while cts < 4 or pend:
                if pend:
                    conv_evac(l, T, pend.pop(0))
                if cts < 4:
                    conv_mm(l, T, False, cts); pend.append(cts); cts += 1
            if not stats_done:
                ln_stats(l, T)
            for ct in range(4):
                ln_apply(l, T, ct, True)

        def sample_cache_prep(l):
            ckf = stg[:, 0, 0:512].rearrange("p (s c) -> p s c", s=4); cvf = stg[:, 1, 0:512].rearrange("p (s c) -> p s c", s=4)
            S.op('pool', DMA(ckf, ck_d[l, :, :, :].rearrange("s k c -> k s c")), writes=['stg0'], dma=True)
            S.op('pool', DMA(cvf, cv_d[l, :, :, :].rearrange("s k c -> k s c")), writes=['stg1'], dma=True)
            S.op('dve', CP(ckb[:], ckf), reads=['stg0'], writes=['ckb'])
            S.op('act', ACPY(vc[:, :, :, 0:64], stg[:, 1, 0:512].rearrange("p (s k d) -> p s k d", s=4, k=2)), reads=['stg1'], writes=['vc'])
            S.op('pe', [TR(PTb[0:64, (s * 2 + kv) * 128:(s * 2 + kv + 1) * 128], ckb[:, s, kv * 64:(kv + 1) * 64], identb[:]) for s in range(4) for kv in range(2)],
                 reads=['ckb', 'identb'], writes=[bk(6), bk(7)])
            S.op('dve', CP(kTc[:], PTb[0:64, 0:1024].rearrange("p (s k t) -> p s k t", s=4, k=2)), reads=[bk(6), bk(7)], writes=['kTc'])
            stf = stg[0:CST, :, :].rearrange("p a (b c) -> p (a b) c", b=2)
            S.op('pool', DMA(stf, stc_d[l, :, :, :].rearrange("s t c -> t s c")), writes=['stg0', 'stg1'], dma=True)
            S.op('pe', [TR(PT[:, (ct * 4 + s) * 32:(ct * 4 + s) * 32 + CST], stf[:, s, ct * 128:(ct + 1) * 128], identf[0:CST, 0:CST]) for ct in range(4) for s in range(4)],
                 reads=['stg0', 'stg1', 'identf'], writes=[bk(6)])
            S.op('dve', CP(usx[:, :, :, 2:32], PT[:, 0:512].rearrange("p (c s t) -> p c s t", c=4, s=4)[:, :, :, 0:CST]), reads=[bk(6)], writes=['usx'])

        def attn_sample(l):
            for kvh in range(2):
                Sc = banks[0]; Sn = banks[1]; Ob = banks[4 + kvh]
                fns = [MM(Sc[:, s * 64:(s + 1) * 64], kTc[:, s, kvh, :], qT[:, kvh * 4:(kvh + 1) * 4, s * 16:(s + 1) * 16], True, True) for s in range(4)]
                S.op('pe', fns, reads=arr(0, 8) + ['kTc'], writes=[bk(0)])
                S.op('pe', MM(Sn[0:64, 0:256], kT[:, l, kvh, 128:192], qT[:, kvh * 4:(kvh + 1) * 4, 0:64], True, True),
                     reads=arr(0, 8) + [('kT', l)], writes=[bk(1)])
                S.op('act', ACT(pTc[:, :], Sc[:, 0:256], AF.Exp, bias=zcol[:], scale=0.125), reads=[bk(0), 'zcol'], writes=['pTc'])
                S.op('act', ACT(pTn[:, :], Sn[0:64, 0:256], AF.Exp, bias=zcol[0:64, :], scale=0.125), reads=[bk(1), 'zcol'], writes=['pTn'])
                pn3 = pTn[:, :].rearrange("p (g q) -> p g q", g=4)
                S.op('dve', TT(pn3, pn3, mskb[:, :].unsqueeze(1).to_broadcast([64, 4, 64]), ALU.mult), reads=['pTn', 'mskb'], writes=['pTn'])
                fns = []
                for s in range(4):
                    fns.append(MM(Ob[:, s * 64:(s + 1) * 64], vc[:, s, kvh, :], pTc[:, s * 64:(s + 1) * 64], True, False))
                    fns.append(MM(Ob[:, s * 64:(s + 1) * 64], vS[0:64, l, 1, kvh, :], pn3[:, :, s * 16:(s + 1) * 16], False, True))
                S.op('pe', fns, reads=['pTc', 'pTn', 'vc', ('vS', l, 1)], writes=[bk(4 + kvh)])
                attn_epilogue(l, kvh, Ob, 4 + kvh, 256, 0, 16, sample=True)

        def wout_phase(l, T):
            wa_, ka_ = wnext(); wc, kc_ = wnext()

            def attn_part(m):
                pb = banks[m % 6]
                S.op('pe', [MM(pb[:, :T], wa_[:, j, m * 128:(m + 1) * 128], OT2[:, j, :T], j == 0, False) for j in range(4)],
                     reads=arr(8, 12) + [ka_], writes=[bk(m % 6)])
            for m in range(6):
                attn_part(m)
            for m in range(8):
                pb = banks[m % 6]
                S.op('pe', [MM(pb[:, :T], wc[:, ct, m * 128:(m + 1) * 128], co[:, ct, :T], False, ct == 3) for ct in range(4)],
                     reads=arr(16, 20) + [kc_], writes=[bk(m % 6)])
                S.op('dve', TT(xT[:, m, :T], xT[:, m, :T], pb[:, :T], ALU.add), reads=[bk(m % 6), ('xT', m)], writes=[('xT', m)])
                if m + 6 < 8:
                    attn_part(m + 6)
                if m >= 1:
                    norm_stats_k(m - 1, T)
            wdone(2)
            norm_stats_k(7, T)
            norm_finish(l, g2T, T)

        def ffn_phase(l, T, pool_sq=True):
            for j in range(8):
                w, kw = wnext()
                for mm in range(4):
                    m = j * 4 + mm
                    ib = m % 4
                    pb = banks[ib]
                    fm = [MM(pb[:, :T], w[:, k, mm * 128:(mm + 1) * 128], hT[:, k, :T], k == 0, k == 7) for k in range(8)]
                    if m == 0:
                        for k in range(8):
                            S.op('pe', fm[k], reads=[('hT', k), kw], writes=[bk(ib)])
                    else:
                        S.op('pe', fm, reads=hTk + [kw], writes=[bk(ib)])
                    if m % 2 == 0:
                        S.op('act', ACT(t_r[:, 0, :T], pb[:, :T], AF.Relu), reads=[bk(ib)], writes=[('t_r', 0)])
                        S.op('dve', TT(ar[:, m, :T], t_r[:, 0, :T], t_r[:, 0, :T], ALU.mult), reads=[('t_r', 0)], writes=[('ar', m)])
                    else:
                        S.op('dve', TS(t_r[:, 1, :T], pb[:, :T], 0.0, None, ALU.max), reads=[bk(ib)], writes=[('t_r', 1)])
                        S.op('pool' if pool_sq else 'dve', TT(ar[:, m, :T], t_r[:, 1, :T], t_r[:, 1, :T], ALU.mult), reads=[('t_r', 1)], writes=[('ar', m)])
                wdone(1)
            for ch in range(2):
                for kc in range(4):
                    w, kw = wnext()
                    fns = []
                    b0 = 4 if ch == 0 else 0
                    for mm in range(4):
                        for kk in range(8):
                            fns.append(MM(banks[b0 + mm][:, :T], w[:, kk, mm * 128:(mm + 1) * 128], ar[:, kc * 8 + kk, :T],
                                          kc == 0 and kk == 0, kc == 3 and kk == 7))
                    S.op('pe', fns, reads=arr(kc * 8, kc * 8 + 8) + [kw], writes=[bk(b0 + i) for i in range(4)])
                    wdone(1)
                for mm in range(4):
                    m = ch * 4 + mm
                    S.op('dve', TT(xT[:, m, :T], xT[:, m, :T], banks[b0 + mm][:, :T], ALU.add), reads=[bk(b0 + mm), ('xT', m)], writes=[('xT', m)])

        def tail_out(l, sample):
            nt = 64 if sample else 32
            S.op('pe', [TR(PT[0:nt, ct * 128:(ct + 1) * 128], u32[:, ct, 0:nt], identf[:]) for ct in range(4)], reads=['u32', 'identf'], writes=[bk(6)])
            S.op('act', ACPY(ostg[0:nt, :], PT[0:nt, 0:512]), reads=[bk(6)], writes=['ostg'])
            if sample:
                for s in range(4):
                    S.op('pool', DMA(sc_d[l, s, 14:30, :], ostg[s * 16:(s + 1) * 16, :]), reads=['ostg'], dma=True)
            else:
                S.op('pool', DMA(pc_d[l, :, :], ostg[2:32, :]), reads=['ostg'], dma=True)

        def hist_shift(l, T, nb, halo):
            S.op('pool', CP(kT[:, l, :, 0:128], kT[:, l, :, T:T + 128]), reads=[('kT', l)], writes=[('kT', l)])
            S.op('pool', CP(vS[:, l, 0, :, 0:64], vS[:, l, nb, :, 0:64]), reads=[('vS', l, nb)], writes=[('vS', l, 0)])
            if halo:
                S.op('pool', TS(uX[:, l, :, 0:32], uX[:, l, :, T:T + 32], hbf[:, 1:2], None, ALU.mult), reads=[('uX', l, c) for c in range(4)] + ['hbf'],
                     writes=[('uX', l, c) for c in range(4)])
            else:
                S.op('pool', CP(uX[:, l, :, 0:32], uX[:, l, :, T:T + 32]), reads=[('uX', l, c) for c in range(4)], writes=[('uX', l, c) for c in range(4)])

        pref = {}
        ldg = dg[:, :, :, :].rearrange("p a b c -> p (a b c)")[:, 0:6144].bitcast(F32).rearrange("p (s c) -> p s c", s=3)
        LDK = [('dg', 0), ('dg', 1)]

        ld3 = sqq2[:, :, :].rearrange("p a b -> p (a b)")[:, 0:1024]
        LD3K = [('sqq', 0), ('sqq', 1)]

        def ldslot(tb):
            if tb % 4 == 3:
                return ld3, LD3K
            return ldg[:, tb % 4, :], LDK

        def prefetch_tile(src, nb, bs):
            for tb in range(min(nb, 4)):
                ap_, keys_ = ldslot(tb)
                S.op('pool', DMA(ap_[:bs, :], src[tb * bs:(tb + 1) * bs, :]), writes=keys_, dma=True)
                pref[tb] = True

        def load_tile(src, T, nb, bs):
            for tb in range(nb):
                s = tb % 2
                c0 = tb * bs
                PX = P67 if s == 0 else P45; bx = [bk(6), bk(7)] if s == 0 else [bk(4), bk(5)]
                ap_, keys_ = ldslot(tb)
                if not pref.pop(tb, False):
                    S.op('pool', DMA(ap_[:bs, :], src[c0:c0 + bs, :]), writes=keys_, dma=True)
                S.op('pe', [TR(PX[:, f * 128:f * 128 + bs], ap_[:bs, f * 128:(f + 1) * 128], identf[:bs, :bs]) for f in range(8)],
                     reads=keys_ + ['identf'], writes=bx)
                S.op('act' if s == 0 else 'dve', (ACPY if s == 0 else CP)(xT[:, :, c0:c0 + bs], PX[:, :].rearrange("p (f t) -> p f t", f=8)[:, :, :bs]), reads=bx,
                     writes=[('xT', k) for k in range(8)])

        def store_tile(dst, T, nb, bs):
            for tb in range(nb):
                s = tb % 2
                c0 = tb * bs
                PX = P67 if s == 0 else P45; bx = [bk(6), bk(7)] if s == 0 else [bk(4), bk(5)]
                S.op('pe', [TR(PX[:bs, f * 128:(f + 1) * 128], xT[:, f, c0:c0 + bs], identf[:]) for f in range(8)],
                     reads=[('xT', k) for k in range(8)] + ['identf'], writes=bx)
                S.op('act' if s == 0 else 'dve', (ACPY if s == 0 else CP)(stg[:bs, s, :], PX[:bs, :]), reads=bx, writes=['stg%d' % s])
                S.op('sp', DMA(dst[c0:c0 + bs, :], stg[:bs, s, :]), reads=['stg%d' % s], dma=True)

        def run_tile(src, dst, T, nb, bs, sample, halo, first_main, last_main, blk0, nxt=None):
            load_tile(src, T, nb, bs)
            for l in range(2):
                want = sample or last_main
                if sample:
                    sample_cache_prep(l)
                norm_phase(l, g1T, T)
                lastB = front_phase(l, T, nb, bs, sample, blk0, want, (64 if sample else 32) if want else 0,
                                    defer_last=(not sample and not halo and nb == 4))
                if dbg == 'glu':
                    raise StopIteration
                if halo and l == 1:
                    hist_shift(l, T, nb, halo)
                    break
                if sample:
                    attn_sample(l)
                    for ct in range(4):
                        conv_mm(l, T, True, ct)
                        conv_evac(l, T, ct)
                    ln_phase(l, T)
                else:
                    mix_prompt(l, T, nb, first_main, lastB)
                if dbg == 'ln':
                    raise StopIteration
                if want:
                    tail_out(l, sample)
                wout_phase(l, T)
                if dbg == 'wout':
                    raise StopIteration
                if l == 1 and nxt is not None:
                    prefetch_tile(*nxt)
                ffn_phase(l, T, pool_sq=not (sample or (halo and l == 0) or (first_main and l == 1)))
                if dbg == 'ffn':
                    raise StopIteration
                if not sample:
                    hist_shift(l, T, nb, halo)
            if dst is not None:
                store_tile(dst, T, nb, bs)

        try:
            run_tile(x_d[0:HALO, :], None, HALO, 2, 128, False, True, False, False, 0)
            for t in range(n_main_tiles):
                if t + 1 < n_main_tiles:
                    nxt = (x_d[HALO + (t + 1) * 512:HALO + (t + 2) * 512, :], 4, 128)
                elif do_sample:
                    nxt = (xs_d, 1, 64)
                else:
                    nxt = None
                run_tile(x_d[HALO + t * 512:HALO + (t + 1) * 512, :], y_d[t * 512:(t + 1) * 512, :], 512, 4, 128, False, False,
                         t == 0, t == n_main_tiles - 1, 2 + t * 4, nxt)
            if do_sample:
                run_tile(xs_d, ys_d, 64, 1, 64, True, False, False, False, 0)
        except StopIteration:
            pass

        block = es.enter_context(nc.Block())

        def emit(e, name):
            for waits, fns, sem, inc in S.streams[name]:
                for s_, v_ in waits:
                    e.wait_ge(sems[s_], v_)
                ins = None
                for f in fns:
                    ins = f(e)
                ins.then_inc(sems[sem], inc)
            if name in ('sp', 'pool'):
                for i in range(NDQ):
                    sn = 'd%s%d' % (name, i)
                    if S.cnt.get(sn, 0):
                        e.wait_ge(sems[sn], S.cnt[sn])

        @block.tensor
        def _(e):
            emit(e, 'pe')

        @block.scalar
        def _(e):
            emit(e, 'act')

        @block.vector
        def _(e):
            emit(e, 'dve')

        @block.gpsimd
        def _(e):
            emit(e, 'pool')

        @block.sync
        def _(e):
            emit(e, 'sp')
    return nc


def _rope_tab(pos):
    half = 8
    inv = np.power(np.float32(THETA), -np.arange(half, dtype=np.float32) * np.float32(2.0) / np.float32(16)).astype(np.float32)
    ang = (pos.astype(np.float32)[:, None] * inv[None, :]).astype(np.float32)
    return np.concatenate([np.cos(ang.astype(np.float64)), np.sin(ang.astype(np.float64))], axis=1).astype(np.float32)


def make_in_maps(x_prompt, x_sample, cache_k, cache_v, state_conv, norm1_g, w_in, q_norm_g, k_norm_g,
                 attn_sinks, conv_w, conv_b, conv_ln_g, conv_ln_b, beta_attn, beta_conv, w_out, norm2_g, w_up, w_down):
    f = lambda a: np.ascontiguousarray(np.asarray(a, dtype=np.float32))
    x_prompt = f(x_prompt); x_sample = f(x_sample)
    shared = {
        "w_in": f(w_in), "w_out": f(w_out), "w_up": f(w_up), "w_down": f(w_down),
        "ident": np.eye(128, dtype=np.float32),
        "g1T": f(np.asarray(norm1_g).reshape(2, 8, 128).transpose(2, 0, 1)),
        "g2T": f(np.asarray(norm2_g).reshape(2, 8, 128).transpose(2, 0, 1)),
        "g10": f(np.broadcast_to(np.concatenate([np.tile(np.asarray(q_norm_g), (1, 8)), np.tile(np.asarray(k_norm_g), (1, 2))], axis=1)[None], (128, 2, 640))),
        "snk": f(np.broadcast_to(np.asarray(attn_sinks)[None], (128, 2, 8))),
        "cwT": f(np.asarray(conv_w).reshape(2, CW, 4, 128).transpose(3, 0, 2, 1)),
        "cvec": f(np.stack([np.asarray(a).reshape(2, 4, 128).transpose(2, 0, 1) for a in (conv_b, conv_ln_g, conv_ln_b, beta_conv)], axis=-1)),
        "ba": f(np.asarray(beta_attn).reshape(2, 8, 64).transpose(2, 0, 1)),
        "css": f(np.tile(_rope_tab(PAST + np.arange(DT)), (4, 1))),
        "msk": f(np.kron(np.eye(4, dtype=np.float32), np.ones((16, 16), np.float32))),
    }
    maps = []
    for c in range(NCORE):
        b, half = c // 2, c % 2
        st = half * MAIN
        xc = np.zeros((NTOK, D), np.float32)
        xc[HALO:] = x_prompt[b, st:st + MAIN]
        if half:
            xc[:HALO] = x_prompt[b, st - HALO:st]
        pos = np.arange(st - HALO, st + MAIN)
        m = dict(shared)
        m["x"] = xc
        m["xs"] = f(x_sample[4 * c:4 * c + 4].reshape(64, D))
        m["ck"] = f(np.asarray(cache_k)[:, 4 * c:4 * c + 4].reshape(2, 4, 128, 128))
        m["cv"] = f(np.asarray(cache_v)[:, 4 * c:4 * c + 4].reshape(2, 4, 128, 128))
        m["stc"] = f(np.asarray(state_conv)[:, 4 * c:4 * c + 4])
        m["csn"] = f(_rope_tab(pos).reshape(NTOK // 128, 128, 16).transpose(1, 0, 2))
        hb = np.zeros((128, 2), np.float32)
        hb[:, 0] = 0.0 if half else -30000.0
        hb[:, 1] = 1.0 if half else 0.0
        m["hbf"] = hb
        maps.append(m)
    return maps


_NC_CACHE = {}


def kernel(**inputs):
    maps = make_in_maps(**inputs)
    if 'nc' not in _NC_CACHE:
        _NC_CACHE['nc'] = build_program()
    nc = _NC_CACHE['nc']
    res = run_bass_kernel_spmd(nc, maps, core_ids=list(range(NCORE))).results
    y = np.zeros((NB, SEQ, D), np.float32)
    ys = np.zeros((DB, DT, D), np.float32)
    pk = np.zeros((2, NB, 128, 2, 64), np.float32); pv = np.zeros_like(pk); pc = np.zeros((2, NB, CST, CCH), np.float32)
    sk = np.zeros((2, DB, 128, 2, 64), np.float32); sv = np.zeros_like(sk); sc = np.zeros((2, DB, CST, CCH), np.float32)
    for c in range(NCORE):
        b, half = c // 2, c % 2
        r = res[c]
        y[b, half * MAIN:(half + 1) * MAIN] = r["y"]
        ys[4 * c:4 * c + 4] = r["ys"].reshape(4, DT, D)
        sk[:, 4 * c:4 * c + 4] = r["sk"].reshape(2, 4, 128, 2, 64)
        sv[:, 4 * c:4 * c + 4] = r["sv"].reshape(2, 4, 128, 2, 64)
        sc[:, 4 * c:4 * c + 4] = r["sc"]
        if half:
            pk[:, b] = r["pk"].reshape(2, 128, 2, 64)
            pv[:, b] = r["pv"].reshape(2, 128, 2, 64)
            pc[:, b] = r["pc"]
    return (y, ys, pk, pv, pc, sk, sv, sc)
```
